# Optimizing a Trainium2 kernel written in Bass

```python
import jax, jax.numpy as jnp
from jax import lax
import numpy as np

D_MODEL = 1024
BATCH = 8
SEQ = 4096
DEPTH = 4
DEC_BATCH = 16
DEC_SEQ = 4096
PAST_LEN = 128

N_MIXERS = 2
N_ATTN_LAYERS = (DEPTH + 1) // 2
N_CONV_LAYERS = DEPTH // 2
HEAD_DIM = 128
N_HEADS = D_MODEL // HEAD_DIM
N_KV_HEADS = 2
GQA_GROUP = N_HEADS // N_KV_HEADS
ATTN_WIDTH = N_HEADS * HEAD_DIM
KV_WIDTH = N_KV_HEADS * HEAD_DIM
ATTN_IN_WIDTH = 2 * ATTN_WIDTH + 2 * KV_WIDTH
ROPE_AXIS_DIM = HEAD_DIM // 2
ROPE_THETA = 10000.0
Q_BLOCK = 128
CONV_WIDTH = D_MODEL
CONV_IN_WIDTH = 3 * CONV_WIDTH
CONV_KERNEL = 31
CONV_PAD = CONV_KERNEL // 2
GRID_W = 64
EPS = 1e-6

kernel_name = "hybrid_axial_gqa_conformer_encoder"


def rms_norm(x, g):
    xf = x.astype(jnp.float32)
    y = xf * lax.rsqrt(jnp.mean(xf * xf, axis=-1, keepdims=True) + EPS)
    return (y * g.astype(jnp.float32)).astype(x.dtype)


def layer_norm(x, g, b):
    xf = x.astype(jnp.float32)
    mu = jnp.mean(xf, axis=-1, keepdims=True)
    xc = xf - mu
    var = jnp.mean(xc * xc, axis=-1, keepdims=True)
    y = xc * lax.rsqrt(var + EPS) * g.astype(jnp.float32) + b.astype(jnp.float32)
    return y.astype(x.dtype)


def axial_rope_tables(seq_len):
    rows = seq_len // GRID_W
    row = jnp.repeat(jnp.arange(rows, dtype=jnp.float32), GRID_W)
    col = jnp.tile(jnp.arange(GRID_W, dtype=jnp.float32), rows)
    inv_freq = ROPE_THETA ** (-jnp.arange(0, ROPE_AXIS_DIM, 2, dtype=jnp.float32) / ROPE_AXIS_DIM)
    ang_r = row[:, None] * inv_freq[None, :]
    ang_c = col[:, None] * inv_freq[None, :]
    ang = jnp.concatenate([ang_r, ang_r, ang_c, ang_c], axis=-1)
    return jnp.cos(ang), jnp.sin(ang)


def _rotate_half(h):
    a, b = jnp.split(h, 2, axis=-1)
    return jnp.concatenate([-b, a], axis=-1)


def apply_axial_rope(x, cos, sin):
    xf = x.astype(jnp.float32)
    xr, xc = jnp.split(xf, 2, axis=-1)
    rot = jnp.concatenate([_rotate_half(xr), _rotate_half(xc)], axis=-1)
    out = xf * cos[None, :, None, :] + rot * sin[None, :, None, :]
    return out.astype(x.dtype)


def attention_mixer(h, w_in, q_g, k_g, w_out, cos, sin):
    B, S, _ = h.shape
    proj = h @ w_in
    q, k, v, z = jnp.split(proj, [ATTN_WIDTH, ATTN_WIDTH + KV_WIDTH, ATTN_WIDTH + 2 * KV_WIDTH], axis=-1)
    q = rms_norm(q.reshape(B, S, N_HEADS, HEAD_DIM), q_g)
    k = rms_norm(k.reshape(B, S, N_KV_HEADS, HEAD_DIM), k_g)
    v = v.reshape(B, S, N_KV_HEADS, HEAD_DIM)
    q = apply_axial_rope(q, cos, sin)
    k = apply_axial_rope(k, cos, sin)
    n_blk = S // Q_BLOCK
    qb = q.reshape(B, n_blk, Q_BLOCK, N_KV_HEADS, GQA_GROUP, HEAD_DIM).transpose(1, 0, 2, 3, 4, 5)
    scale = HEAD_DIM ** -0.5

    def one_block(q_blk):
        s = jnp.einsum('bqkgd,bskd->bkgqs', q_blk, k, preferred_element_type=jnp.float32) * scale
        p = jax.nn.softmax(s, axis=-1).astype(v.dtype)
        return jnp.einsum('bkgqs,bskd->bqkgd', p, v)

    o = lax.map(one_block, qb)
    o = o.transpose(1, 0, 2, 3, 4, 5).reshape(B, S, ATTN_WIDTH)
    o = o * jax.nn.silu(z)
    return o @ w_out


def conv_mixer(h, w_in, dw_w, dw_b, ln_g, ln_b, w_out):
    proj = h @ w_in
    a, g, z = jnp.split(proj, 3, axis=-1)
    u = a * jax.nn.sigmoid(g)
    u = lax.conv_general_dilated(
        u, dw_w[:, None, :], window_strides=(1,), padding=[(CONV_PAD, CONV_PAD)],
        dimension_numbers=('NWC', 'WIO', 'NWC'), feature_group_count=CONV_WIDTH) + dw_b
    u = jax.nn.silu(layer_norm(u, ln_g, ln_b))
    u = u * jax.nn.silu(z)
    return u @ w_out


def trunk(x, pre_norm_g, post_norm_g, attn_w_in, attn_q_norm_g, attn_k_norm_g, attn_w_out,
          conv_w_in, conv_dw_w, conv_dw_b, conv_ln_g, conv_ln_b, conv_w_out):
    cos, sin = axial_rope_tables(x.shape[1])
    for i in range(DEPTH):
        j = i // N_MIXERS
        h = rms_norm(x, pre_norm_g[i])
        if i % N_MIXERS == 0:
            m = attention_mixer(h, attn_w_in[j], attn_q_norm_g[j], attn_k_norm_g[j], attn_w_out[j], cos, sin)
        else:
            m = conv_mixer(h, conv_w_in[j], conv_dw_w[j], conv_dw_b[j], conv_ln_g[j], conv_ln_b[j], conv_w_out[j])
        x = x + rms_norm(m, post_norm_g[i])
    return x


def setup_inputs(seed: int = 0) -> dict:
    key = jax.random.key(seed)
    ks = jax.random.split(key, 16)
    f32 = jnp.float32
    nrm = lambda k, shape, s: jax.random.normal(k, shape, f32) * s
    return {
        "x_prompt": nrm(ks[0], (BATCH, SEQ, D_MODEL), 1.0),
        "x_sample": nrm(ks[1], (DEC_BATCH, DEC_SEQ, D_MODEL), 1.0),
        "pre_norm_g": 1.0 + nrm(ks[2], (DEPTH, D_MODEL), 0.02),
        "post_norm_g": 1.0 + nrm(ks[3], (DEPTH, D_MODEL), 0.02),
        "attn_w_in": nrm(ks[4], (N_ATTN_LAYERS, D_MODEL, ATTN_IN_WIDTH), D_MODEL ** -0.5),
        "attn_q_norm_g": 1.0 + nrm(ks[5], (N_ATTN_LAYERS, HEAD_DIM), 0.02),
        "attn_k_norm_g": 1.0 + nrm(ks[6], (N_ATTN_LAYERS, HEAD_DIM), 0.02),
        "attn_w_out": nrm(ks[7], (N_ATTN_LAYERS, ATTN_WIDTH, D_MODEL), ATTN_WIDTH ** -0.5),
        "conv_w_in": nrm(ks[8], (N_CONV_LAYERS, D_MODEL, CONV_IN_WIDTH), D_MODEL ** -0.5),
        "conv_dw_w": nrm(ks[9], (N_CONV_LAYERS, CONV_KERNEL, CONV_WIDTH), CONV_KERNEL ** -0.5),
        "conv_dw_b": nrm(ks[10], (N_CONV_LAYERS, CONV_WIDTH), 0.02),
        "conv_ln_g": 1.0 + nrm(ks[11], (N_CONV_LAYERS, CONV_WIDTH), 0.02),
        "conv_ln_b": nrm(ks[12], (N_CONV_LAYERS, CONV_WIDTH), 0.02),
        "conv_w_out": nrm(ks[13], (N_CONV_LAYERS, CONV_WIDTH, D_MODEL), CONV_WIDTH ** -0.5),
    }


def reference(x_prompt, x_sample, pre_norm_g, post_norm_g, attn_w_in, attn_q_norm_g, attn_k_norm_g,
              attn_w_out, conv_w_in, conv_dw_w, conv_dw_b, conv_ln_g, conv_ln_b, conv_w_out):
    y_prompt = trunk(x_prompt, pre_norm_g, post_norm_g, attn_w_in, attn_q_norm_g, attn_k_norm_g, attn_w_out,
                     conv_w_in, conv_dw_w, conv_dw_b, conv_ln_g, conv_ln_b, conv_w_out)
    y_sample = trunk(x_sample, pre_norm_g, post_norm_g, attn_w_in, attn_q_norm_g, attn_k_norm_g, attn_w_out,
                     conv_w_in, conv_dw_w, conv_dw_b, conv_ln_g, conv_ln_b, conv_w_out)
    return (y_prompt, y_sample)
```

```python
import numpy as np
from contextlib import ExitStack
import concourse.bass as bass
import concourse.mybir as mybir
from concourse.bass_utils import run_bass_kernel_spmd

F32 = mybir.dt.float32
BF16 = mybir.dt.bfloat16
AF = mybir.ActivationFunctionType
ALU = mybir.AluOpType
AX = mybir.AxisListType

D = 1024
S = 4096
NT = S // 128
NB = S // 512
HD = 128
NH = 8
NKV = 2
KW = 31
PAD = 15
EPS = 1e-6
N_CORES = 8
SEQ_PER_CORE = 3
ROPE_THETA = 10000.0
GRID_W = 64
HW = 512 + 2 * PAD


class Buf:
    __slots__ = ("name", "w", "r", "dsem", "dcnt", "_wdma")

    def __init__(self, name):
        self.name = name
        self.w = None
        self.r = {}
        self.dsem = None
        self.dcnt = 0
        self._wdma = False


class Eng:
    def __init__(self, kb, name, e, self_wait=True):
        self.kb, self.name, self.e = kb, name, e
        self.sem = None
        self.cnt = 0
        self.seen = {}
        self.nep = 0
        self.self_wait = self_wait

    def epoch(self):
        self.nep += 1
        self.sem = self.kb.new_sem(f"{self.name}e{self.nep}")
        self.cnt = 0

    def mark(self, ins):
        self.cnt += 1
        ins.then_inc(self.sem, 1)
        return (self.sem, self.cnt, self)

    def wait(self, toks):
        for t in toks:
            if t is None:
                continue
            sem, v, owner = t
            if owner is self and not self.self_wait:
                continue
            k = id(sem)
            if self.seen.get(k, 0) >= v:
                continue
            self.seen[k] = v
            self.e.wait_ge(sem, v)


class KB:
    def __init__(self, nc, ctx):
        self.nc, self.ctx = nc, ctx
        self.nsem = 0
        self.pe = Eng(self, "pe", nc.tensor, self_wait=False)
        self.act = Eng(self, "act", nc.scalar)
        self.dve = Eng(self, "dve", nc.vector)
        self.pool = Eng(self, "pool", nc.gpsimd)
        self.sp = Eng(self, "sp", nc.sync)
        self.engs = [self.pe, self.act, self.dve, self.pool, self.sp]
        self.dma_bufs = []
        for e in self.engs:
            e.epoch()

    def new_sem(self, name):
        self.nsem += 1
        return self.ctx.enter_context(self.nc.semaphore(f"{name}_{self.nsem}"))

    def sb(self, name, shape, dt):
        return self.ctx.enter_context(self.nc.sbuf_tensor(name, shape, dt))

    @staticmethod
    def _deps(reads, writes):
        deps = []
        for b in reads:
            deps.append(b.w)
        for b in writes:
            deps.append(b.w)
            deps.extend(b.r.values())
        return deps

    @staticmethod
    def _note(tok, reads, writes):
        for b in reads:
            b.r[id(tok[0])] = tok
        for b in writes:
            b.w = tok
            b.r = {}

    def op(self, eng, fns, reads=(), writes=()):
        eng.wait(self._deps(reads, writes))
        if callable(fns):
            fns = [fns]
        ins = None
        for f in fns:
            ins = f()
        tok = eng.mark(ins)
        self._note(tok, reads, writes)
        for b in writes:
            b._wdma = False
        return tok

    def dma(self, eng, out, in_, sbuf, reads=(), writes=(), disjoint=False):
        if sbuf.dsem is None:
            sbuf.dsem = self.new_sem("d_" + sbuf.name)
            self.dma_bufs.append(sbuf)
        deps = []
        for b in reads:
            deps.append(b.w)
        for b in writes:
            if b.w is not None and not (disjoint and b.w[0] is sbuf.dsem and b is sbuf and b._wdma):
                deps.append(b.w)
            deps.extend(b.r.values())
        eng.wait(deps)
        sbuf.dcnt += 16
        eng.e.dma_start(out=out, in_=in_).then_inc(sbuf.dsem, 16)
        tok = (sbuf.dsem, sbuf.dcnt, None)
        self._note(tok, reads, writes)
        for b in writes:
            b._wdma = True
        return tok

    def barrier(self):
        toks = [(e.sem, e.cnt, None) for e in self.engs if e.cnt > 0]
        toks += [(b.dsem, b.dcnt, None) for b in self.dma_bufs]
        for e in self.engs:
            e.wait(toks)


def build_program(nseq=SEQ_PER_CORE, layers=(0, 1, 2, 3)):
    nc = bass.Bass("TRN2", target_bir_lowering=False)
    dr = lambda n, s, k: nc.dram_tensor(n, list(s), F32, kind=k).ap()
    xin = dr("xin", [nseq, S, D], "ExternalInput")
    yout = dr("yout", [nseq, S, D], "ExternalOutput")
    scrA = dr("scrA", [nseq, S, D], "Internal")
    scrB = dr("scrB", [nseq, S, D], "Internal")
    gpre_d = dr("gpre_b", [4, 128, D], "ExternalInput")
    gpost_d = dr("gpost_b", [4, 128, D], "ExternalInput")
    awin_d = dr("attn_w_in", [2, D, 2560], "ExternalInput")
    awout_d = dr("attn_w_out", [2, D, D], "ExternalInput")
    gqk_d = dr("gqk_b", [2, 4, 128, 128], "ExternalInput")
    cwin_d = dr("conv_w_in", [2, D, 3072], "ExternalInput")
    cwout_d = dr("conv_w_out", [2, D, D], "ExternalInput")
    cvec_d = dr("conv_vec", [2, 128, 8, 34], "ExternalInput")
    cs_d = dr("cs_t", [S, 2, 128], "ExternalInput")
    idn_d = dr("ident", [128, 128], "ExternalInput")

    ctx = ExitStack()
    with ctx:
        kb = KB(nc, ctx)
        pe, act, dve, pool, sp = kb.pe, kb.act, kb.dve, kb.pool, kb.sp
        PE, ACT, DVE, POOL = nc.tensor, nc.scalar, nc.vector, nc.gpsimd

        ps = ctx.enter_context(nc.psum_tensor("ps", [128, 4096], F32))
        bank = [Buf(f"bank{i}") for i in range(8)]
        psf = lambda b, n=1: ps[:, b * 512:(b + n) * 512]
        psb = lambda b: ps[:, b * 512:(b + 1) * 512].bitcast(BF16)

        ident = kb.sb("ident_sb", [128, 128], BF16); ident_b = Buf("ident")
        ones = kb.sb("ones", [128, 128], BF16); ones_b = Buf("ones")
        mhalf = kb.sb("mhalf", [128, 16], F32); mhalf_b = Buf("mhalf")
        gpre = kb.sb("gpre", [128, D], F32); gpre_b = Buf("gpre")
        gpost = kb.sb("gpost", [128, D], F32); gpost_b = Buf("gpost")
        WA_b = Buf("WA")
        WZ = kb.sb("WZ", [128, 8, 1024], BF16); WZ_b = Buf("WZ")
        WO = kb.sb("WO", [128, 8, 1024], BF16); WO_b = Buf("WO")
        NX = 3
        xt = [kb.sb(f"xt{i}", [128, D], F32) for i in range(NX)]; xt_b = [Buf(f"xt{i}") for i in range(NX)]
        hb = [kb.sb(f"hb{i}", [128, D], BF16) for i in range(2)]; hb_b = [Buf(f"hb{i}") for i in range(2)]
        sq = kb.sb("sq", [128, D], BF16); sq_b = Buf("sq")
        NST = 16
        st = [kb.sb(f"st{i}", [128, 8], F32) for i in range(NST)]; st_b = [Buf(f"st{i}") for i in range(NST)]
        tmp = kb.sb("tmp", [128, D], F32); tmp_b = Buf("tmp")
        yt = [kb.sb(f"yt{i}", [128, D], F32) for i in range(2)]; yt_b = [Buf(f"yt{i}") for i in range(2)]
        GT = kb.sb("GT", [128, 8, 512], BF16); GT_b = [Buf(f"GT{i}") for i in range(8)]
        SZ = kb.sb("SZ", [128, 8, 512], BF16); SZ_b = [Buf(f"SZ{i}") for i in range(8)]

        kb.dma(pool, ident[:], idn_d, ident_b, writes=[ident_b])
        kb.op(dve, lambda: DVE.memset(ones[:], 1.0), writes=[ones_b])
        kb.op(dve, lambda: DVE.memset(mhalf[:], -0.5), writes=[mhalf_b])

        cnt = {"x": 0, "hb": 0, "st": 0, "y": 0}

        def dram_bufs(tag):
            return [[Buf(f"{tag}_{s}_{t}") for t in range(NT)] for s in range(nseq)]
        dbufs = {id(xin): dram_bufs("xin"), id(scrA): dram_bufs("scrA"), id(scrB): dram_bufs("scrB"),
                 id(yout): dram_bufs("yout")}

        def load_weights(dst, dst_b, src, col0, ncol, dcol0=0):
            for kc in range(8):
                kb.dma(pool, dst[:, kc, dcol0:dcol0 + ncol], src[kc * 128:(kc + 1) * 128, col0:col0 + ncol],
                       dst_b, writes=[dst_b], disjoint=True)

        def rstd_from_ss(stt, stb, c0, n, inv_n):
            v = stt[:, c0:c0 + n]
            kb.op(act, lambda: ACT.activation(out=v, in_=v, func=AF.Ln, scale=inv_n, bias=EPS),
                  reads=[stb], writes=[stb])
            kb.op(act, lambda: ACT.activation(out=v, in_=v, func=AF.Exp, scale=-0.5),
                  reads=[stb], writes=[stb])

        def prenorm_a(src, s, t):
            i = cnt["x"] % NX; cnt["x"] += 1
            X, Xb = xt[i], xt_b[i]
            kb.dma(sp, X[:], src[s, t * 128:(t + 1) * 128, :], Xb, reads=[dbufs[id(src)][s][t]], writes=[Xb])
            j = cnt["st"] % NST; cnt["st"] += 1
            T, Tb = st[j], st_b[j]
            kb.op(act, lambda: ACT.activation(out=sq[:], in_=X[:], func=AF.Square, accum_out=T[:, 0:1]),
                  reads=[Xb], writes=[sq_b, Tb])
            rstd_from_ss(T, Tb, 0, 1, 1.0 / D)
            h = cnt["hb"] % 2; cnt["hb"] += 1
            H, Hb = hb[h], hb_b[h]
            kb.op(dve, lambda: DVE.scalar_tensor_tensor(out=H[:], in0=X[:], scalar=T[:, 0:1], in1=gpre[:],
                                                        op0=ALU.mult, op1=ALU.mult),
                  reads=[Xb, Tb, gpre_b], writes=[Hb])
            return H, Hb

        def prenorm_b(H, Hb, dstT, dst_bufs, col, pbank):
            pv = psb(pbank)
            kb.op(pe, [(lambda c=c: PE.transpose(pv[:, c * 128:(c + 1) * 128], H[:, c * 128:(c + 1) * 128], ident[:]))
                       for c in range(8)],
                  reads=[Hb, ident_b], writes=[bank[pbank]])
            kb.op(dve, lambda: DVE.tensor_copy(out=dstT[:, :, col:col + 128],
                                               in_=pv.rearrange("p (c t) -> p c t", c=8)),
                  reads=[bank[pbank]], writes=dst_bufs)

        def prenorm_tile(src, s, t, dstT, dst_bufs, col, pbank):
            H, Hb = prenorm_a(src, s, t)
            prenorm_b(H, Hb, dstT, dst_bufs, col, pbank)

        def out_tile_a(tt, pb0):
            for hf in range(2):
                kb.op(pe, [(lambda kc=kc: PE.matmul(psf(pb0 + hf), lhsT=GT[:, kc, tt * 128:(tt + 1) * 128],
                                                    rhs=WO[:, kc, hf * 512:(hf + 1) * 512],
                                                    start=(kc == 0), stop=(kc == 7))) for kc in range(8)],
                      reads=GT_b + [WO_b], writes=[bank[pb0 + hf]])

        def out_tile_b(src, dst, s, t, pb0):
            mps = psf(pb0, 2)
            j = cnt["st"] % NST; cnt["st"] += 1
            T, Tb = st[j], st_b[j]
            kb.op(act, lambda: ACT.activation(out=sq[:], in_=mps, func=AF.Square, accum_out=T[:, 0:1]),
                  reads=[bank[pb0], bank[pb0 + 1]], writes=[sq_b, Tb])
            rstd_from_ss(T, Tb, 0, 1, 1.0 / D)
            kb.op(dve, lambda: DVE.scalar_tensor_tensor(out=tmp[:], in0=mps, scalar=T[:, 0:1], in1=gpost[:],
                                                        op0=ALU.mult, op1=ALU.mult),
                  reads=[bank[pb0], bank[pb0 + 1], Tb, gpost_b], writes=[tmp_b])
            i = cnt["x"] % NX; cnt["x"] += 1
            X, Xb = xt[i], xt_b[i]
            kb.dma(sp, X[:], src[s, t * 128:(t + 1) * 128, :], Xb, reads=[dbufs[id(src)][s][t]], writes=[Xb])
            k = cnt["y"] % 2; cnt["y"] += 1
            Y, Yb = yt[k], yt_b[k]
            kb.op(dve, lambda: DVE.tensor_tensor(out=Y[:], in0=tmp[:], in1=X[:], op=ALU.add),
                  reads=[tmp_b, Xb], writes=[Yb])
            kb.dma(sp, dst[s, t * 128:(t + 1) * 128, :], Y[:], Yb, reads=[Yb], writes=[dbufs[id(dst)][s][t]])

        def out_tile(src, dst, s, t, tt, pb0):
            out_tile_a(tt, pb0)
            out_tile_b(src, dst, s, t, pb0)

        def attn_layer(L, j, src, dst, lctx):
            lsb = lambda n, shp, dt: lctx.enter_context(nc.sbuf_tensor(f"{n}_L{L}", shp, dt))
            WA = lsb("WA", [128, 8, 1536], BF16)
            KT = lsb("KT", [128, NKV, S], BF16); KT_b = [Buf(f"KT{t}") for t in range(NT)]
            V = lsb("V", [128, NT, 256], BF16); V_b = [Buf(f"V{t}") for t in range(NT)]
            gqk = lsb("gqk", [128, 4, 128], F32); gqk_b = Buf("gqk")
            HT = lsb("HT", [128, 8, 512], BF16); HT_b = [Buf(f"HT{i}") for i in range(4)]
            QT2 = [lsb(f"QT{i}", [128, 8, 512], BF16) for i in range(2)]
            QT2_b = [[Buf(f"QT{i}_{k}") for k in range(4)] for i in range(2)]
            QW = 512
            qf = lsb("qf", [128, QW], F32); qf_b = Buf("qf")
            t1 = lsb("t1", [128, QW], F32); t1_b = Buf("t1")
            t2 = lsb("t2", [128, QW], F32); t2_b = Buf("t2")
            qr = lsb("qr", [128, QW], BF16); qr_b = Buf("qr")
            NCS = 3
            cs = [lsb(f"cs{i}", [128, 2, 128], F32) for i in range(NCS)]; cs_b = [Buf(f"cs{i}") for i in range(NCS)]
            csq = [lsb(f"csq{i}", [128, 2, 128], F32) for i in range(2)]; csq_b = [Buf(f"csq{i}") for i in range(2)]
            qcnt = [0]
            kf = lsb("kf", [128, 2, 256], F32); kf_b = [Buf(f"kf{i}") for i in range(2)]
            k1 = lsb("k1", [128, 2, 256], F32); k1_b = [Buf(f"k1{i}") for i in range(2)]
            k2 = lsb("k2", [128, 2, 256], F32); k2_b = [Buf(f"k2{i}") for i in range(2)]
            kr = lsb("kr", [128, 2, 256], BF16); kr_b = [Buf(f"kr{i}") for i in range(2)]
            NPT = 4
            PT = [lsb(f"PT{i}", [128, 1024], BF16) for i in range(NPT)]; PT_b = [Buf(f"PT{i}") for i in range(NPT)]
            ld = lsb("ld", [128, 512], F32); ld_b = Buf("ld")
            on = lsb("on", [128, 512], F32); on_b = Buf("on")
            ze = lsb("ze", [128, 512], F32); ze_b = Buf("ze")
            NPS = 4
            PS2 = [lsb(f"PS{i}", [128, 512], BF16) for i in range(NPS)]; PS2_b = [Buf(f"PS{i}") for i in range(NPS)]

            load_weights(WA, WA_b, awin_d[j], 0, 1536)
            load_weights(WZ, WZ_b, awin_d[j], 1536, 1024)
            load_weights(WO, WO_b, awout_d[j], 0, 1024)
            kb.dma(sp, gpre[:], gpre_d[L], gpre_b, writes=[gpre_b])
            kb.dma(sp, gpost[:], gpost_d[L], gpost_b, writes=[gpost_b])
            kb.dma(sp, gqk[:], gqk_d[j].rearrange("a p d -> p a d"), gqk_b, writes=[gqk_b])
            ccnt = [0]

            def rope_tables(t, which):
                i = ccnt[0] % NCS; ccnt[0] += 1
                C, Cb = cs[i], cs_b[i]
                kb.dma(sp, C[:, 0:2, :], cs_d[t * 128:(t + 1) * 128, :, :], Cb, writes=[Cb])
                kb.op(pool, lambda: POOL.tensor_tensor(out=C[:, 2:4, :], in0=C[:, 0:2, :],
                                                       in1=gqk[:, 2 * which:2 * which + 2, :], op=ALU.mult),
                      reads=[Cb, gqk_b], writes=[Cb])
                return C, Cb

            def norm_rope(src_ps, src_banks, nh, C, Cb):
                W = nh * 128
                j2 = cnt["st"] % NST; cnt["st"] += 1
                T, Tb = st[j2], st_b[j2]
                kb.op(act, lambda: ACT.activation(out=sq[:, 0:W], in_=src_ps, func=AF.Square),
                      reads=src_banks, writes=[sq_b])
                kb.op(act, lambda: ACT.activation(out=qf[:, 0:W], in_=src_ps, func=AF.Copy),
                      reads=src_banks, writes=[qf_b])
                kb.op(dve, lambda: DVE.tensor_reduce(out=T[:, 0:nh], in_=sq[:, 0:W].rearrange("p (h d) -> p h d", h=nh),
                                                     op=ALU.add, axis=AX.X),
                      reads=[sq_b], writes=[Tb])
                rstd_from_ss(T, Tb, 0, nh, 1.0 / HD)
                q3 = qf[:, 0:W].rearrange("p (h d) -> p h d", h=nh)
                kb.op(dve, lambda: DVE.tensor_tensor(out=q3, in0=q3, in1=T[:, 0:nh].unsqueeze(2).to_broadcast([128, nh, 128]),
                                                     op=ALU.mult),
                      reads=[qf_b, Tb], writes=[qf_b])
                kb.op(dve, lambda: DVE.tensor_tensor(out=t1[:, 0:W].rearrange("p (h d) -> p h d", h=nh), in0=q3,
                                                     in1=C[:, 2:3, :].to_broadcast([128, nh, 128]), op=ALU.mult),
                      reads=[qf_b, Cb], writes=[t1_b])
                q4 = qf[:, 0:W].rearrange("p (h r f e) -> p h r f e", h=nh, r=2, f=2)
                o4 = t2[:, 0:W].rearrange("p (h r f e) -> p h r f e", h=nh, r=2, f=2)
                s4 = C[:, 3, :].rearrange("p (r f e) -> p r f e", r=2, f=2)
                fns = []
                for f in range(2):
                    fns.append(lambda f=f: POOL.tensor_tensor(
                        out=o4[:, :, :, f, :], in0=q4[:, :, :, 1 - f, :],
                        in1=s4[:, :, f, :].unsqueeze(1).to_broadcast([128, nh, 2, 32]), op=ALU.mult))
                kb.op(pool, fns, reads=[qf_b, Cb], writes=[t2_b])
                kb.op(dve, lambda: DVE.tensor_tensor(out=qr[:, 0:W], in0=t1[:, 0:W], in1=t2[:, 0:W], op=ALU.add),
                      reads=[t1_b, t2_b], writes=[qr_b])

            scale = float(HD) ** -0.5
            for s in range(nseq):
                PB = 6

                def prologue_stages(b, stride=17):
                    nb = b % 2
                    stages = []
                    for tt in range(4):
                        t = 4 * b + tt
                        o = tt * stride
                        stv = {}

                        def L(t=t, stv=stv):
                            i = cnt["x"] % NX; cnt["x"] += 1
                            stv["X"] = (xt[i], xt_b[i])
                            kb.dma(sp, xt[i][:], src[s, t * 128:(t + 1) * 128, :], xt_b[i],
                                   reads=[dbufs[id(src)][s][t]], writes=[xt_b[i]])
                            ci = qcnt[0] % 2; qcnt[0] += 1
                            stv["C"] = (csq[ci], csq_b[ci])
                            kb.dma(sp, csq[ci][:], cs_d[t * 128:(t + 1) * 128, :, :], csq_b[ci], writes=[csq_b[ci]])

                        def N1(stv=stv):
                            X, Xb = stv["X"]
                            j2 = cnt["st"] % NST; cnt["st"] += 1
                            T, Tb = st[j2], st_b[j2]
                            stv["T"] = (T, Tb)
                            kb.op(act, lambda: ACT.activation(out=sq[:], in_=X[:], func=AF.Square, accum_out=T[:, 0:1]),
                                  reads=[Xb], writes=[sq_b, Tb])

                        def N1b(stv=stv):
                            T, Tb = stv["T"]
                            rstd_from_ss(T, Tb, 0, 1, 1.0 / D)

                        def N2(stv=stv):
                            X, Xb = stv["X"]; T, Tb = stv["T"]; C, Cb = stv["C"]
                            h = cnt["hb"] % 2; cnt["hb"] += 1
                            H, Hb = hb[h], hb_b[h]
                            stv["H"] = (H, Hb)
                            kb.op(dve, lambda: DVE.scalar_tensor_tensor(out=H[:], in0=X[:], scalar=T[:, 0:1], in1=gpre[:],
                                                                        op0=ALU.mult, op1=ALU.mult),
                                  reads=[Xb, Tb, gpre_b], writes=[Hb])
                            kb.op(pool, lambda: POOL.tensor_tensor(out=C[:], in0=C[:], in1=gqk[:, 0:2, :], op=ALU.mult),
                                  reads=[Cb, gqk_b], writes=[Cb])

                        def T1(stv=stv):
                            H, Hb = stv["H"]
                            pv = psb(PB)
                            kb.op(pe, [(lambda c=c: PE.transpose(pv[:, c * 128:(c + 1) * 128], H[:, c * 128:(c + 1) * 128], ident[:]))
                                       for c in range(8)], reads=[Hb, ident_b], writes=[bank[PB]])

                        def C1(tt=tt):
                            pv = psb(PB)
                            kb.op(dve, lambda: DVE.tensor_copy(out=HT[:, :, tt * 128:(tt + 1) * 128],
                                                               in_=pv.rearrange("p (c t) -> p c t", c=8)),
                                  reads=[bank[PB]], writes=[HT_b[tt]])
                        stages += [(o + 0, L), (o + 4, N1), (o + 5, N1b), (o + 6, N2), (o + 7, T1), (o + 8, C1)]
                        for hf in range(2):
                            oo = o + 9 + 5 * hf

                            def Q1(tt=tt, hf=hf):
                                kb.op(pe, [(lambda kc=kc: PE.matmul(psf(PB + 1), lhsT=HT[:, kc, tt * 128:(tt + 1) * 128],
                                                                    rhs=WA[:, kc, hf * 512:(hf + 1) * 512],
                                                                    start=(kc == 0), stop=(kc == 7))) for kc in range(8)],
                                      reads=[HT_b[tt], WA_b], writes=[bank[PB + 1]])

                            def R1a(stv=stv):
                                j2 = cnt["st"] % NST; cnt["st"] += 1
                                T, Tb = st[j2], st_b[j2]
                                stv["T2"] = (T, Tb)
                                src_ps = psf(PB + 1)
                                kb.op(act, [(lambda h=h: ACT.activation(out=sq[:, h * 128:(h + 1) * 128], in_=src_ps[:, h * 128:(h + 1) * 128],
                                                                        func=AF.Square, accum_out=T[:, h:h + 1])) for h in range(2)],
                                      reads=[bank[PB + 1]], writes=[sq_b, Tb])

                            def R1b(stv=stv):
                                T, Tb = stv["T2"]
                                src_ps = psf(PB + 1)
                                kb.op(act, [(lambda h=h: ACT.activation(out=sq[:, h * 128:(h + 1) * 128], in_=src_ps[:, h * 128:(h + 1) * 128],
                                                                        func=AF.Square, accum_out=T[:, h:h + 1])) for h in range(2, 4)],
                                      reads=[bank[PB + 1]], writes=[sq_b, Tb])

                            def R1c(stv=stv):
                                kb.op(act, lambda: ACT.activation(out=qf[:], in_=psf(PB + 1), func=AF.Copy),
                                      reads=[bank[PB + 1]], writes=[qf_b])

                            def R1d(stv=stv):
                                T, Tb = stv["T2"]
                                rstd_from_ss(T, Tb, 0, 4, 1.0 / HD)

                            def R2(stv=stv):
                                T, Tb = stv["T2"]; C, Cb = stv["C"]
                                q3 = qf[:].rearrange("p (h d) -> p h d", h=4)
                                kb.op(dve, lambda: DVE.tensor_tensor(out=q3, in0=q3, in1=T[:, 0:4].unsqueeze(2).to_broadcast([128, 4, 128]),
                                                                     op=ALU.mult), reads=[qf_b, Tb], writes=[qf_b])
                                kb.op(dve, lambda: DVE.tensor_tensor(out=t1[:].rearrange("p (h d) -> p h d", h=4), in0=q3,
                                                                     in1=C[:, 0:1, :].to_broadcast([128, 4, 128]), op=ALU.mult),
                                      reads=[qf_b, Cb], writes=[t1_b])

                            def R2b(stv=stv):
                                C, Cb = stv["C"]
                                q4 = qf[:].rearrange("p (h r f e) -> p h r f e", h=4, r=2, f=2)
                                o4 = t2[:].rearrange("p (h r f e) -> p h r f e", h=4, r=2, f=2)
                                s4 = C[:, 1, :].rearrange("p (r f e) -> p r f e", r=2, f=2)
                                kb.op(pool, [(lambda f=f: POOL.tensor_tensor(
                                    out=o4[:, :, :, f, :], in0=q4[:, :, :, 1 - f, :],
                                    in1=s4[:, :, f, :].unsqueeze(1).to_broadcast([128, 4, 2, 32]), op=ALU.mult)) for f in range(2)],
                                      reads=[qf_b, Cb], writes=[t2_b])

                            def R3():
                                kb.op(dve, lambda: DVE.tensor_tensor(out=qr[:], in0=t1[:], in1=t2[:], op=ALU.add),
                                      reads=[t1_b, t2_b], writes=[qr_b])

                            def T2():
                                pv = psb(PB)
                                kb.op(pe, [(lambda h=h: PE.transpose(pv[:, h * 128:(h + 1) * 128], qr[:, h * 128:(h + 1) * 128], ident[:]))
                                           for h in range(4)], reads=[qr_b, ident_b], writes=[bank[PB]])

                            def C2(tt=tt, hf=hf):
                                pv = psb(PB)
                                kb.op(dve, lambda: DVE.tensor_copy(out=QT2[nb][:, 4 * hf:4 * hf + 4, tt * 128:(tt + 1) * 128],
                                                                   in_=pv[:, 0:512].rearrange("p (h t) -> p h t", h=4)),
                                      reads=[bank[PB]], writes=[QT2_b[nb][tt]])
                            stages += [(oo, Q1), (oo + 2, R1a), (oo + 3, R1b), (oo + 4, R1c), (oo + 5, R1d), (oo + 6, R2), (oo + 6, R2b),
                                       (oo + 7, R3), (oo + 8, T2), (oo + 9, C2)]
                    stages.sort(key=lambda x: x[0])
                    return stages

                def z_stages(fc):
                    zb = PB + (fc % 2)

                    def Z1():
                        kb.op(pe, [(lambda kc=kc: PE.matmul(psf(zb), lhsT=WZ[:, kc, fc * 128:(fc + 1) * 128], rhs=HT[:, kc, :],
                                                            start=(kc == 0), stop=(kc == 7))) for kc in range(8)],
                              reads=HT_b + [WZ_b], writes=[bank[zb]])

                    def Z2a():
                        kb.op(act, lambda: ACT.activation(out=ze[:], in_=psf(zb), func=AF.Exp, scale=-1.0),
                              reads=[bank[zb]], writes=[ze_b])

                    def Z2b():
                        kb.op(act, lambda: ACT.activation(out=ze[:], in_=ze[:], func=AF.Ln, bias=1.0), reads=[ze_b], writes=[ze_b])

                    def Z2c():
                        kb.op(act, lambda: ACT.activation(out=ze[:], in_=ze[:], func=AF.Exp, scale=-1.0), reads=[ze_b], writes=[ze_b])

                    def Z3():
                        kb.op(dve, lambda: DVE.tensor_tensor(out=SZ[:, fc, :], in0=psf(zb), in1=ze[:], op=ALU.mult),
                              reads=[bank[zb], ze_b], writes=[SZ_b[fc]])
                    return [(0, Z1), (2, Z2a), (3, Z2b), (4, Z2c), (5, Z3)]

                def epilogue_stages(b, tt):
                    t = 4 * b + tt
                    stv = {}

                    def Ea():
                        i = cnt["x"] % NX; cnt["x"] += 1
                        stv["X"] = (xt[i], xt_b[i])
                        kb.dma(sp, xt[i][:], src[s, t * 128:(t + 1) * 128, :], xt_b[i],
                               reads=[dbufs[id(src)][s][t]], writes=[xt_b[i]])
                        out_tile_a(tt, PB)

                    def Eb1():
                        j2 = cnt["st"] % NST; cnt["st"] += 1
                        T, Tb = st[j2], st_b[j2]
                        stv["T"] = (T, Tb)
                        kb.op(act, lambda: ACT.activation(out=sq[:], in_=psf(PB, 2), func=AF.Square, accum_out=T[:, 0:1]),
                              reads=[bank[PB], bank[PB + 1]], writes=[sq_b, Tb])

                    def Eb1b():
                        T, Tb = stv["T"]
                        rstd_from_ss(T, Tb, 0, 1, 1.0 / D)

                    def Eb2():
                        T, Tb = stv["T"]
                        kb.op(dve, lambda: DVE.scalar_tensor_tensor(out=tmp[:], in0=psf(PB, 2), scalar=T[:, 0:1], in1=gpost[:],
                                                                    op0=ALU.mult, op1=ALU.mult),
                              reads=[bank[PB], bank[PB + 1], Tb, gpost_b], writes=[tmp_b])

                    def Eb3():
                        X, Xb = stv["X"]
                        k = cnt["y"] % 2; cnt["y"] += 1
                        Y, Yb = yt[k], yt_b[k]
                        kb.op(dve, lambda: DVE.tensor_tensor(out=Y[:], in0=tmp[:], in1=X[:], op=ALU.add),
                              reads=[tmp_b, Xb], writes=[Yb])
                        kb.dma(sp, dst[s, t * 128:(t + 1) * 128, :], Y[:], Yb, reads=[Yb], writes=[dbufs[id(dst)][s][t]])
                    return [(0, Ea), (3, Eb1), (4, Eb1b), (5, Eb2), (6, Eb3)]

                items = [(h, g) for h in range(NH) for g in range(16)]
                cur = {}

                def emit_S(b, n):
                    h, g = items[n]
                    sl = n % 2
                    kvh = h // 4
                    QT, QT_b = QT2[b % 2], QT2_b[b % 2]
                    kb.op(pe, [(lambda i=i: PE.matmul(psf(2 * sl + i), lhsT=KT[:, kvh, (2 * g + i) * 128:(2 * g + i + 1) * 128],
                                                      rhs=QT[:, h, :], start=True, stop=True)) for i in range(2)],
                          reads=QT_b + [KT_b[2 * g], KT_b[2 * g + 1]], writes=[bank[2 * sl], bank[2 * sl + 1]])

                def emit_den(n):
                    h, g = items[n]
                    if g % 4 != 3:
                        return
                    Q, Qb = PS2[n % NPS], PS2_b[n % NPS]
                    kb.op(pe, lambda: PE.matmul(psf(5), lhsT=ones[:], rhs=Q[:], start=(g == 3), stop=(g == 15)),
                          reads=[Qb, ones_b], writes=[bank[5]])
                    if g == 15:
                        kb.op(dve, lambda: DVE.tensor_copy(out=on[:], in_=psf(4)), reads=[bank[4]], writes=[on_b])
                        kb.op(act, lambda: ACT.activation(out=ld[:], in_=psf(5), func=AF.Ln), reads=[bank[5]], writes=[ld_b])
                        def fin_exp():
                            kb.op(act, lambda: ACT.activation(out=ld[:], in_=ld[:], func=AF.Exp, scale=-1.0),
                                  reads=[ld_b], writes=[ld_b])

                        def fin_mul():
                            kb.op(dve, lambda: DVE.tensor_tensor(out=on[:], in0=on[:], in1=ld[:], op=ALU.mult),
                                  reads=[on_b, ld_b], writes=[on_b])

                        def gt_write(h=h):
                            kb.op(pool, lambda: POOL.tensor_tensor(out=GT[:, h, :], in0=on[:], in1=SZ[:, h, :], op=ALU.mult),
                                  reads=[on_b, SZ_b[h]], writes=[GT_b[h]])
                        pos = cur["n"]
                        cur["at"](pos + 1, fin_exp)
                        cur["at"](pos + 2, fin_mul)
                        if h == 0 and cur["b"] > 0:
                            cur["at"](max(27, pos + 2), gt_write)
                        else:
                            cur["at"](pos + 2, gt_write)

                def emit_rest(b, n):
                    h, g = items[n]
                    sl = n % 2
                    kvh = h // 4
                    P, Pb = PT[n % NPT], PT_b[n % NPT]
                    Q, Qb = PS2[n % NPS], PS2_b[n % NPS]
                    kb.op(act, lambda: ACT.activation(out=P[:], in_=psf(2 * sl, 2), func=AF.Exp, scale=scale),
                          reads=[bank[2 * sl], bank[2 * sl + 1]], writes=[Pb])
                    kb.op(dve, lambda: DVE.tensor_tensor(out=Q[:], in0=P[:, 0:512], in1=P[:, 512:1024], op=ALU.add),
                          reads=[Pb], writes=[Qb])
                    if g % 2 == 1:
                        Qp, Qpb = PS2[(n - 1) % NPS], PS2_b[(n - 1) % NPS]
                        kb.op(dve, lambda: DVE.tensor_tensor(out=Q[:], in0=Q[:], in1=Qp[:], op=ALU.add),
                              reads=[Qb, Qpb], writes=[Qb])
                    if g % 4 == 3:
                        Qq, Qqb = PS2[(n - 2) % NPS], PS2_b[(n - 2) % NPS]
                        kb.op(dve, lambda: DVE.tensor_tensor(out=Q[:], in0=Q[:], in1=Qq[:], op=ALU.add),
                              reads=[Qb, Qqb], writes=[Qb])
                    cur["n"] = n
                    if g == 0:
                        if n >= 2:
                            emit_den(n - 2)
                        if n >= 1:
                            emit_den(n - 1)
                    elif g >= 2:
                        emit_den(n - 2)
                    fns = []
                    for i in range(2):
                        kc = 2 * g + i
                        fns.append(lambda i=i, kc=kc: PE.matmul(psf(4), lhsT=V[:, kc, kvh * 128:(kvh + 1) * 128],
                                                                rhs=P[:, i * 512:(i + 1) * 512],
                                                                start=(kc == 0), stop=(kc == NT - 1)))
                    kb.op(pe, fns, reads=[Pb, V_b[2 * g], V_b[2 * g + 1]], writes=[bank[4]])

                stA = {}

                def aL(t):
                    i = cnt["x"] % NX; cnt["x"] += 1
                    stA[(t, "X")] = (xt[i], xt_b[i])
                    kb.dma(sp, xt[i][:], src[s, t * 128:(t + 1) * 128, :], xt_b[i],
                           reads=[dbufs[id(src)][s][t]], writes=[xt_b[i]])

                def aN1(t):
                    X, Xb = stA[(t, "X")]
                    j2 = cnt["st"] % NST; cnt["st"] += 1
                    T, Tb = st[j2], st_b[j2]
                    stA[(t, "T")] = (T, Tb)
                    kb.op(act, lambda: ACT.activation(out=sq[:], in_=X[:], func=AF.Square, accum_out=T[:, 0:1]),
                          reads=[Xb], writes=[sq_b, Tb])
                    rstd_from_ss(T, Tb, 0, 1, 1.0 / D)

                def aN2(t):
                    X, Xb = stA.pop((t, "X")); T, Tb = stA.pop((t, "T"))
                    h = cnt["hb"] % 2; cnt["hb"] += 1
                    H, Hb = hb[h], hb_b[h]
                    stA[(t, "H")] = (H, Hb)
                    kb.op(dve, lambda: DVE.scalar_tensor_tensor(out=H[:], in0=X[:], scalar=T[:, 0:1], in1=gpre[:],
                                                                op0=ALU.mult, op1=ALU.mult),
                          reads=[Xb, Tb, gpre_b], writes=[Hb])

                def aT1(t):
                    H, Hb = stA.pop((t, "H"))
                    pv = psb(t % 2)
                    kb.op(pe, [(lambda c=c: PE.transpose(pv[:, c * 128:(c + 1) * 128], H[:, c * 128:(c + 1) * 128], ident[:]))
                               for c in range(8)], reads=[Hb, ident_b], writes=[bank[t % 2]])

                def aC1(t):
                    par = t % 2
                    kb.op(dve, lambda: DVE.tensor_copy(out=HT[:, :, par * 128:(par + 1) * 128],
                                                       in_=psb(par).rearrange("p (c t) -> p c t", c=8)),
                          reads=[bank[par]], writes=[HT_b[par]])

                def aK1(t):
                    par = t % 2
                    kb.op(pe, [(lambda kc=kc: PE.matmul(psf(2 + par), lhsT=HT[:, kc, par * 128:(par + 1) * 128],
                                                        rhs=WA[:, kc, 1024:1536], start=(kc == 0), stop=(kc == 7)))
                               for kc in range(8)], reads=[HT_b[par], WA_b], writes=[bank[2 + par]])
                    ci = ccnt[0] % NCS; ccnt[0] += 1
                    stA[(t, "C")] = (cs[ci], cs_b[ci])
                    kb.dma(sp, cs[ci][:], cs_d[t * 128:(t + 1) * 128, :, :], cs_b[ci], writes=[cs_b[ci]])

                def aR1(t):
                    par = t % 2
                    kps = psf(2 + par)
                    C, Cb = stA[(t, "C")]
                    kb.op(pool, lambda: POOL.tensor_tensor(out=C[:], in0=C[:], in1=gqk[:, 2:4, :], op=ALU.mult),
                          reads=[Cb, gqk_b], writes=[Cb])
                    kb.op(act, lambda: ACT.activation(out=V[:, t, :], in_=kps[:, 256:512], func=AF.Copy),
                          reads=[bank[2 + par]], writes=[V_b[t]])
                    j2 = cnt["st"] % NST; cnt["st"] += 1
                    T, Tb = st[j2], st_b[j2]
                    stA[(t, "T2")] = (T, Tb)
                    kb.op(act, [(lambda h=h: ACT.activation(out=sq[:, h * 128:(h + 1) * 128], in_=kps[:, h * 128:(h + 1) * 128],
                                                            func=AF.Square, accum_out=T[:, h:h + 1])) for h in range(2)],
                          reads=[bank[2 + par]], writes=[sq_b, Tb])
                    kb.op(act, lambda: ACT.activation(out=kf[:, par, :], in_=kps[:, 0:256], func=AF.Copy),
                          reads=[bank[2 + par]], writes=[kf_b[par]])
                    rstd_from_ss(T, Tb, 0, 2, 1.0 / HD)

                def aR2(t):
                    par = t % 2
                    T, Tb = stA.pop((t, "T2")); C, Cb = stA.pop((t, "C"))
                    q3 = kf[:, par, :].rearrange("p (h d) -> p h d", h=2)
                    kb.op(dve, lambda: DVE.tensor_tensor(out=q3, in0=q3, in1=T[:, 0:2].unsqueeze(2).to_broadcast([128, 2, 128]),
                                                         op=ALU.mult), reads=[kf_b[par], Tb], writes=[kf_b[par]])
                    kb.op(dve, lambda: DVE.tensor_tensor(out=k1[:, par, :].rearrange("p (h d) -> p h d", h=2), in0=q3,
                                                         in1=C[:, 0:1, :].to_broadcast([128, 2, 128]), op=ALU.mult),
                          reads=[kf_b[par], Cb], writes=[k1_b[par]])
                    q4 = kf[:, par, :].rearrange("p (h r f e) -> p h r f e", h=2, r=2, f=2)
                    o4 = k2[:, par, :].rearrange("p (h r f e) -> p h r f e", h=2, r=2, f=2)
                    s4 = C[:, 1, :].rearrange("p (r f e) -> p r f e", r=2, f=2)
                    kb.op(pool, [(lambda f=f: POOL.tensor_tensor(
                        out=o4[:, :, :, f, :], in0=q4[:, :, :, 1 - f, :],
                        in1=s4[:, :, f, :].unsqueeze(1).to_broadcast([128, 2, 2, 32]), op=ALU.mult)) for f in range(2)],
                          reads=[kf_b[par], Cb], writes=[k2_b[par]])

                def aR3(t):
                    par = t % 2
                    kb.op(dve, lambda: DVE.tensor_tensor(out=kr[:, par, :], in0=k1[:, par, :], in1=k2[:, par, :], op=ALU.add),
                          reads=[k1_b[par], k2_b[par]], writes=[kr_b[par]])

                def aT2(t):
                    par = t % 2
                    pv = psb(4 + par)
                    kb.op(pe, [(lambda g=g: PE.transpose(pv[:, g * 128:(g + 1) * 128], kr[:, par, g * 128:(g + 1) * 128], ident[:]))
                               for g in range(2)], reads=[kr_b[par], ident_b], writes=[bank[4 + par]])

                def aC2(t):
                    par = t % 2
                    kb.op(dve, lambda: DVE.tensor_copy(out=KT[:, :, t * 128:(t + 1) * 128],
                                                       in_=psb(4 + par)[:, 0:256].rearrange("p (g t) -> p g t", g=2)),
                          reads=[bank[4 + par]], writes=[KT_b[t]])

                a_stages = [aL, aN1, aN2, aT1, aC1, aK1, aR1, aR2, aR3, aT2, aC2]
                NIT = NT + len(a_stages) - 1
                pro0 = prologue_stages(0)
                pi = 0
                for it in range(NIT):
                    for k in range(len(a_stages) - 1, -1, -1):
                        t = it - k
                        if 0 <= t < NT:
                            a_stages[k](t)
                while pi < len(pro0):
                    pro0[pi][1]()
                    pi += 1

                for fc in range(8):
                    for _, f in z_stages(fc):
                        f()

                for b in range(NB):
                    sched = {}

                    def at(pos, f):
                        sched.setdefault(min(pos, 127), []).append(f)
                    cur["b"] = b
                    cur["at"] = at
                    if b > 0:
                        for o, f in z_stages(7):
                            at(0 + o, f)
                        for tt in range(4):
                            for o, f in epilogue_stages(b - 1, tt):
                                at(6 + 6 * tt + o, f)
                    if b + 1 < NB:
                        for o, f in prologue_stages(b + 1):
                            at(31 + o, f)
                        for fc in range(7):
                            for o, f in z_stages(fc):
                                at(107 + 3 * fc + o, f)
                    emit_S(b, 0)
                    for n in range(len(items)):
                        if n + 1 < len(items):
                            emit_S(b, n + 1)
                        emit_rest(b, n)
                        if n == len(items) - 1:
                            cur["n"] = n + 1
                            emit_den(n - 1)
                            emit_den(n)
                        for f in sched.get(n, []):
                            f()
                for tt in range(4):
                    for _, f in epilogue_stages(NB - 1, tt):
                        f()

        def conv_layer(L, j, src, dst, lctx):
            lsb = lambda n, shp, dt: lctx.enter_context(nc.sbuf_tensor(f"{n}_L{L}", shp, dt))
            WA = lsb("WA", [128, 8, 2048], BF16)
            DG = [lsb(f"DG{i}", [128, KW, 128], BF16) for i in range(2)]; DG_b = [Buf(f"DG{i}") for i in range(2)]
            cv = lsb("cv", [128, 8, 34], F32); cv_b = Buf("cv")
            idf = lsb("idf", [128, 128], F32); idf_b = Buf("idf")
            Hh = [lsb(f"Hh{i}", [128, 8, HW], BF16) for i in range(2)]
            Hh_b = [[Buf(f"Hh{i}_{k}") for k in range(6)] for i in range(2)]
            U = lsb("U", [128, 8, HW], BF16); U_b = [Buf(f"U{c}") for c in range(8)]
            th = [lsb(f"th{i}", [128, HW], F32) for i in range(2)]; th_b = [Buf(f"th{i}") for i in range(2)]
            vv = lsb("vv", [128, 8, 512], F32); vv_b = [Buf(f"vv{c}") for c in range(8)]
            vb = [lsb(f"vb{i}", [128, 512], BF16) for i in range(2)]; vb_b = [Buf(f"vb{i}") for i in range(2)]
            v2 = [lsb(f"v2{i}", [128, 512], BF16) for i in range(2)]; v2_b = [Buf(f"v2{i}") for i in range(2)]
            mu = lsb("mu", [128, 512], F32); mu_b = Buf("mu")
            rs = lsb("rs", [128, 512], F32); rs_b = Buf("rs")
            w1 = [lsb(f"w1{i}", [128, 512], F32) for i in range(2)]; w1_b = [Buf(f"w1{i}") for i in range(2)]
            w2 = [lsb(f"w2{i}", [128, 512], F32) for i in range(2)]; w2_b = [Buf(f"w2{i}") for i in range(2)]
            dgd = nc.dram_tensor(f"dgd_L{L}", [8, 128, KW * 128], BF16, kind="Internal").ap()
            dgd_b = [Buf(f"dgd{c}") for c in range(8)]

            load_weights(WA, WA_b, cwin_d[j], 0, 2048)
            load_weights(WZ, WZ_b, cwin_d[j], 2048, 1024)
            load_weights(WO, WO_b, cwout_d[j], 0, 1024)
            kb.dma(sp, gpre[:], gpre_d[L], gpre_b, writes=[gpre_b])
            kb.dma(sp, gpost[:], gpost_d[L], gpost_b, writes=[gpost_b])
            kb.dma(sp, cv[:], cvec_d[j], cv_b, writes=[cv_b])
            kb.dma(sp, idf[:], idn_d, idf_b, writes=[idf_b])
            for c in range(8):
                G, Gb = DG[c % 2], DG_b[c % 2]
                kb.op(dve, [(lambda k=k: DVE.tensor_scalar(out=G[:, k, :], in0=idf[:], scalar1=cv[:, c, k:k + 1],
                                                           scalar2=0.5, op0=ALU.mult, op1=ALU.mult)) for k in range(KW)],
                      reads=[idf_b, cv_b], writes=[Gb])
                kb.dma(sp, dgd[c], G[:].rearrange("p k m -> p (k m)"), dgd_b[c], reads=[Gb], writes=[dgd_b[c]])
            gidx = [0]

            def dg_load(g):
                c = g % 8
                kb.dma(pool, DG[g % 2][:].rearrange("p k m -> p (k m)"), dgd[c], DG_b[g % 2],
                       reads=[dgd_b[c]], writes=[DG_b[g % 2]])

            GBL = [(s_, b_) for s_ in range(nseq) for b_ in range(NB)]
            NG = len(GBL)
            if True:
                def prep_stages(g):
                    s, b = GBL[g]
                    i = g % 2
                    hs = {}

                    def A(tt):
                        hs[tt] = prenorm_a(src, s, 4 * b + tt)

                    def B(tt):
                        H_, Hb_ = hs.pop(tt)
                        prenorm_b(H_, Hb_, Hh[i], [Hh_b[i][tt]], PAD + tt * 128, 6 + tt % 2)
                    return [[lambda: A(0), lambda: A(1)], [lambda: B(0), lambda: A(2)], [lambda: B(1), lambda: A(3)],
                            [lambda: B(2)], [lambda: B(3)]]

                def prep_block(g):
                    for st_ in prep_stages(g):
                        for f in st_:
                            f()

                def halo_copies(g):
                    i = g % 2
                    kb.op(pool, lambda: POOL.tensor_copy(out=Hh[i][:, :, PAD + 512:HW], in_=Hh[1 - i][:, :, PAD:2 * PAD]),
                          reads=[Hh_b[1 - i][0]], writes=[Hh_b[i][5]])
                    kb.op(pool, lambda: POOL.tensor_copy(out=Hh[1 - i][:, :, 0:PAD], in_=Hh[i][:, :, 512:512 + PAD]),
                          reads=[Hh_b[i][3]], writes=[Hh_b[1 - i][4]])

                def glu(g):
                    s, b = GBL[g]
                    H, Hb = Hh[g % 2], Hh_b[g % 2]
                    has = (b > 0, b + 1 < NB)

                    def pe_part(c):
                        ab, gb_ = 2 * (c % 2), 2 * (c % 2) + 1
                        hbk = 6 + (c % 2)

                        def mm(dst_ps, col0, kc, rhs):
                            return PE.matmul(dst_ps, lhsT=WA[:, kc, col0 + c * 128:col0 + (c + 1) * 128], rhs=rhs,
                                             start=(kc == 0), stop=(kc == 7))
                        kb.op(pe, [(lambda kc=kc: mm(psf(ab), 0, kc, H[:, kc, PAD:PAD + 512])) for kc in range(8)],
                              reads=Hb[0:4] + [WA_b], writes=[bank[ab]])
                        kb.op(pe, [(lambda kc=kc: mm(psf(gb_), 1024, kc, H[:, kc, PAD:PAD + 512])) for kc in range(8)],
                              reads=Hb[0:4] + [WA_b], writes=[bank[gb_]])
                        for side, hc0 in ((0, 0), (1, PAD + 512)):
                            if not has[side]:
                                continue
                            pa = psf(hbk)[:, side * 128:side * 128 + PAD]
                            pg = psf(hbk)[:, side * 128 + 32:side * 128 + 32 + PAD]
                            kb.op(pe, [(lambda kc=kc: mm(pa, 0, kc, H[:, kc, hc0:hc0 + PAD])) for kc in range(8)] +
                                      [(lambda kc=kc: mm(pg, 1024, kc, H[:, kc, hc0:hc0 + PAD])) for kc in range(8)],
                                  reads=[Hb[4 + side], WA_b], writes=[bank[hbk]])

                    def rest(c):
                        ab, gb_ = 2 * (c % 2), 2 * (c % 2) + 1
                        hbk = 6 + (c % 2)
                        T, Tb = th[c % 2], th_b[c % 2]
                        kb.op(act, lambda: ACT.activation(out=T[:, PAD:PAD + 512], in_=psf(gb_), func=AF.Tanh, scale=0.5),
                              reads=[bank[gb_]], writes=[Tb])
                        for side, hc0 in ((0, 0), (1, PAD + 512)):
                            if not has[side]:
                                kb.op(pool, lambda: POOL.memset(U[:, c, hc0:hc0 + PAD], 0.0), writes=[U_b[c]])
                                continue
                            pg = psf(hbk)[:, side * 128 + 32:side * 128 + 32 + PAD]
                            kb.op(act, lambda: ACT.activation(out=T[:, hc0:hc0 + PAD], in_=pg, func=AF.Tanh, scale=0.5),
                                  reads=[bank[hbk]], writes=[Tb])
                        kb.op(dve, lambda: DVE.scalar_tensor_tensor(out=U[:, c, PAD:PAD + 512], in0=T[:, PAD:PAD + 512], scalar=1.0,
                                                                    in1=psf(ab), op0=ALU.add, op1=ALU.mult),
                              reads=[Tb, bank[ab]], writes=[U_b[c]])
                        for side, hc0 in ((0, 0), (1, PAD + 512)):
                            if not has[side]:
                                continue
                            pa = psf(hbk)[:, side * 128:side * 128 + PAD]
                            kb.op(dve, lambda: DVE.scalar_tensor_tensor(
                                out=U[:, c, hc0:hc0 + PAD], in0=T[:, hc0:hc0 + PAD], scalar=1.0, in1=pa, op0=ALU.add, op1=ALU.mult),
                                  reads=[Tb, bank[hbk]], writes=[U_b[c]])
                    return pe_part, rest

                def glu_plain(g):
                    pe_part, rest = glu(g)
                    for c in range(9):
                        if c < 8:
                            pe_part(c)
                        if c >= 1:
                            rest(c - 1)

                def zgate(g):
                    H, Hb = Hh[g % 2], Hh_b[g % 2]
                    for fc in range(9):
                        if fc < 8:
                            zb = 6 + fc % 2
                            kb.op(pe, [(lambda kc=kc: PE.matmul(psf(zb), lhsT=WZ[:, kc, fc * 128:(fc + 1) * 128],
                                                                rhs=H[:, kc, PAD:PAD + 512], start=(kc == 0), stop=(kc == 7)))
                                       for kc in range(8)], reads=Hb[0:4] + [WZ_b], writes=[bank[zb]])
                        if fc >= 1:
                            f1 = fc - 1
                            kb.op(act, lambda: ACT.activation(out=SZ[:, f1, :], in_=psf(6 + f1 % 2), func=AF.Silu),
                                  reads=[bank[6 + f1 % 2]], writes=[SZ_b[f1]])

                def dwconv(first, side):
                    if first:
                        dg_load(gidx[0]); dg_load(gidx[0] + 1)
                    g0 = gidx[0]

                    def v_act(c):
                        cb = c % 4
                        kb.op(act, lambda: ACT.activation(out=vv[:, c, :], in_=psf(cb), func=AF.Identity, bias=cv[:, c, 31:32]),
                              reads=[bank[cb], cv_b], writes=[vv_b[c]])
                        kb.op(act, lambda: ACT.activation(out=v2[c % 2][:], in_=psf(cb), func=AF.Square, bias=cv[:, c, 31:32]),
                              reads=[bank[cb], cv_b], writes=[v2_b[c % 2]])
                        kb.op(act, lambda: ACT.activation(out=vb[c % 2][:], in_=psf(cb), func=AF.Identity, bias=cv[:, c, 31:32]),
                              reads=[bank[cb], cv_b], writes=[vb_b[c % 2]])

                    def v_st(c):
                        kb.op(pe, lambda: PE.matmul(psf(4), lhsT=ones[:], rhs=vb[c % 2][:], start=(c == 0), stop=(c == 7)),
                              reads=[vb_b[c % 2], ones_b], writes=[bank[4]])
                        kb.op(pe, lambda: PE.matmul(psf(5), lhsT=ones[:], rhs=v2[c % 2][:], start=(c == 0), stop=(c == 7)),
                              reads=[v2_b[c % 2], ones_b], writes=[bank[5]])
                    for i in range(10):
                        if i < 8:
                            c = i
                            g = g0 + c
                            G, Gb = DG[g % 2], DG_b[g % 2]
                            kb.op(pe, [(lambda k=k: PE.matmul(psf(c % 4), lhsT=G[:, k, :], rhs=U[:, c, k:k + 512],
                                                              start=(k == 0), stop=(k == KW - 1))) for k in range(KW)],
                                  reads=[U_b[c], Gb], writes=[bank[c % 4]])
                            dg_load(g + 2)
                            if side and i < len(side):
                                for f in side[i]:
                                    f()
                        if i >= 2:
                            v_st(i - 2)
                        if 1 <= i <= 8:
                            v_act(i - 1)
                    gidx[0] += 8

                def ln_norm(nxt):
                    gpe, grest = glu(nxt) if nxt is not None else (None, None)
                    kb.op(act, lambda: ACT.activation(out=mu[:], in_=psf(4), func=AF.Copy, scale=1.0 / D),
                          reads=[bank[4]], writes=[mu_b])
                    kb.op(dve, lambda: DVE.tensor_tensor(out=rs[:], in0=mu[:], in1=mu[:], op=ALU.mult),
                          reads=[mu_b], writes=[rs_b])
                    kb.op(dve, lambda: DVE.scalar_tensor_tensor(out=rs[:], in0=psf(5), scalar=1.0 / D, in1=rs[:],
                                                                op0=ALU.mult, op1=ALU.subtract),
                          reads=[bank[5], rs_b], writes=[rs_b])
                    kb.op(dve, lambda: DVE.tensor_scalar(out=rs[:], in0=rs[:], scalar1=0.0, scalar2=EPS, op0=ALU.max, op1=ALU.add),
                          reads=[rs_b], writes=[rs_b])
                    if gpe:
                        gpe(0)
                    kb.op(act, lambda: ACT.activation(out=rs[:], in_=rs[:], func=AF.Ln), reads=[rs_b], writes=[rs_b])
                    kb.op(act, lambda: ACT.activation(out=rs[:], in_=rs[:], func=AF.Exp, scale=-0.5), reads=[rs_b], writes=[rs_b])
                    if gpe:
                        gpe(1)
                        grest(0)

                    def nrm_a(c):
                        A, Ab = w1[c % 2], w1_b[c % 2]
                        kb.op(dve, lambda: DVE.tensor_tensor(out=A[:], in0=vv[:, c, :], in1=mu[:], op=ALU.subtract),
                              reads=[vv_b[c], mu_b], writes=[Ab])
                        kb.op(dve, lambda: DVE.tensor_tensor(out=A[:], in0=A[:], in1=rs[:], op=ALU.mult),
                              reads=[Ab, rs_b], writes=[Ab])

                    def nrm_b(c1):
                        A, Ab = w1[c1 % 2], w1_b[c1 % 2]
                        B, Bb = w2[c1 % 2], w2_b[c1 % 2]
                        kb.op(act, lambda: ACT.activation(out=B[:], in_=A[:], func=AF.Silu, scale=cv[:, c1, 32:33], bias=cv[:, c1, 33:34]),
                              reads=[Ab, cv_b], writes=[Bb])
                        kb.op(pool, lambda: POOL.tensor_tensor(out=GT[:, c1, :], in0=B[:], in1=SZ[:, c1, :], op=ALU.mult),
                              reads=[Bb, SZ_b[c1]], writes=[GT_b[c1]])
                    for c in range(9):
                        if c < 8:
                            nrm_a(c)
                        if c >= 1:
                            nrm_b(c - 1)
                        if gpe:
                            if c + 2 < 8:
                                gpe(c + 2)
                            if c + 1 < 8:
                                grest(c + 1)

                def outs(g):
                    s, b = GBL[g]
                    for tt in range(5):
                        if tt < 4:
                            out_tile_a(tt, 2 * (tt % 2))
                        if tt >= 1:
                            out_tile_b(src, dst, s, 4 * b + tt - 1, 2 * ((tt - 1) % 2))

                prep_block(0)
                prep_block(1)
                halo_copies(0)
                glu_plain(0)
                zgate(0)
                for g in range(NG):
                    side = prep_stages(g + 2) if g + 2 < NG else None
                    dwconv(first=(g == 0), side=side)
                    if g + 2 < NG and GBL[g + 1][0] == GBL[g + 2][0]:
                        halo_copies(g + 1)
                    ln_norm(g + 1 if g + 1 < NG else None)
                    outs(g)
                    if g + 1 < NG:
                        zgate(g + 1)

        chain = [xin, scrA, scrB, scrA, yout]
        for li, L in enumerate(layers):
            src = chain[L] if len(layers) == 4 else (xin if li == 0 else [scrA, scrB][(li - 1) % 2])
            dst = chain[L + 1] if len(layers) == 4 else (yout if li == len(layers) - 1 else [scrA, scrB][li % 2])
            if li > 0:
                kb.barrier()
                for e in kb.engs:
                    e.epoch()
            with ExitStack() as lctx:
                if L % 2 == 0:
                    attn_layer(L, L // 2, src, dst, lctx)
                else:
                    conv_layer(L, L // 2, src, dst, lctx)
                kb.barrier()
    return nc


def _rope_tables():
    rows = S // GRID_W
    row = np.repeat(np.arange(rows, dtype=np.float32), GRID_W)
    col = np.tile(np.arange(GRID_W, dtype=np.float32), rows)
    inv_freq = (ROPE_THETA ** (-np.arange(0, 64, 2, dtype=np.float32) / 64.0)).astype(np.float32)
    ang_r = row[:, None] * inv_freq[None, :]
    ang_c = col[:, None] * inv_freq[None, :]
    ang = np.concatenate([ang_r, ang_r, ang_c, ang_c], axis=-1).astype(np.float32)
    cos = np.cos(ang).astype(np.float32)
    sin = np.sin(ang).astype(np.float32)
    sgn = np.concatenate([-np.ones(32), np.ones(32), -np.ones(32), np.ones(32)]).astype(np.float32)
    return cos, (sin * sgn[None, :]).astype(np.float32)


_SWAP = np.concatenate([np.arange(32, 64), np.arange(0, 32), np.arange(96, 128), np.arange(64, 96)])


def _host_layout(inp):
    f = lambda a: np.ascontiguousarray(np.asarray(a, dtype=np.float32))
    bc = lambda v: np.ascontiguousarray(np.broadcast_to(np.asarray(v, np.float32)[:, None, :], (v.shape[0], 128, v.shape[1])))
    gq, gk = np.asarray(inp["attn_q_norm_g"], np.float32), np.asarray(inp["attn_k_norm_g"], np.float32)
    gqk = np.stack([gq, gq[:, _SWAP], gk, gk[:, _SWAP]], axis=1)
    gqk_b = np.ascontiguousarray(np.broadcast_to(gqk[:, :, None, :], (2, 4, 128, 128)))
    dw = np.asarray(inp["conv_dw_w"], np.float32)
    vec = np.concatenate([dw, np.asarray(inp["conv_dw_b"], np.float32)[:, None, :],
                          np.asarray(inp["conv_ln_g"], np.float32)[:, None, :],
                          np.asarray(inp["conv_ln_b"], np.float32)[:, None, :]], axis=1)
    cvec = np.ascontiguousarray(vec.reshape(2, 34, 8, 128).transpose(0, 3, 2, 1))
    cos, sin = _rope_tables()
    return {
        "gpre_b": bc(inp["pre_norm_g"]), "gpost_b": bc(inp["post_norm_g"]),
        "attn_w_in": f(inp["attn_w_in"]), "attn_w_out": f(inp["attn_w_out"]), "gqk_b": gqk_b,
        "conv_w_in": f(inp["conv_w_in"]), "conv_w_out": f(inp["conv_w_out"]), "conv_vec": cvec,
        "cs_t": np.ascontiguousarray(np.stack([cos, sin], axis=1)), "ident": np.eye(128, dtype=np.float32),
    }


def kernel(**inputs):
    xp = np.asarray(inputs["x_prompt"], np.float32)
    xs = np.asarray(inputs["x_sample"], np.float32)
    xall = np.concatenate([xp, xs], axis=0)
    shared = _host_layout(inputs)
    nc = build_program()
    in_maps = []
    for c in range(N_CORES):
        m = dict(shared)
        m["xin"] = np.ascontiguousarray(xall[c * SEQ_PER_CORE:(c + 1) * SEQ_PER_CORE])
        in_maps.append(m)
    res = run_bass_kernel_spmd(nc, in_maps, core_ids=list(range(N_CORES)))
    yall = np.concatenate([np.asarray(r["yout"], np.float32) for r in res.results], axis=0)
    return (np.ascontiguousarray(yall[:xp.shape[0]]), np.ascontiguousarray(yall[xp.shape[0]:]))
```

```python
import numpy as np
from contextlib import ExitStack
import concourse.bass as bass
import concourse.mybir as mybir
from concourse.bass_utils import run_bass_kernel_spmd

F32 = mybir.dt.float32
BF16 = mybir.dt.bfloat16
AF = mybir.ActivationFunctionType
ALU = mybir.AluOpType
AX = mybir.AxisListType

D = 1024
S = 4096
NT = S // 128
NB = S // 512
HD = 128
NH = 8
NKV = 2
KW = 31
PAD = 15
EPS = 1e-6
N_CORES = 8
SEQ_PER_CORE = 3
ROPE_THETA = 10000.0
GRID_W = 64
HW = 512 + 2 * PAD


class Buf:
    __slots__ = ("name", "w", "r", "dsem", "dcnt", "_wdma")

    def __init__(self, name):
        self.name = name
        self.w = None
        self.r = {}
        self.dsem = None
        self.dcnt = 0
        self._wdma = False


class Eng:
    def __init__(self, kb, name, e, self_wait=True):
        self.kb, self.name, self.e = kb, name, e
        self.sem = None
        self.cnt = 0
        self.seen = {}
        self.nep = 0
        self.self_wait = self_wait

    def epoch(self):
        self.nep += 1
        self.sem = self.kb.new_sem(f"{self.name}e{self.nep}")
        self.cnt = 0

    def mark(self, ins):
        self.cnt += 1
        ins.then_inc(self.sem, 1)
        return (self.sem, self.cnt, self)

    def wait(self, toks):
        for t in toks:
            if t is None:
                continue
            sem, v, owner = t
            if owner is self and not self.self_wait:
                continue
            k = id(sem)
            if self.seen.get(k, 0) >= v:
                continue
            self.seen[k] = v
            self.e.wait_ge(sem, v)


class KB:
    def __init__(self, nc, ctx):
        self.nc, self.ctx = nc, ctx
        self.nsem = 0
        self.pe = Eng(self, "pe", nc.tensor, self_wait=False)
        self.act = Eng(self, "act", nc.scalar)
        self.dve = Eng(self, "dve", nc.vector)
        self.pool = Eng(self, "pool", nc.gpsimd)
        self.sp = Eng(self, "sp", nc.sync)
        self.engs = [self.pe, self.act, self.dve, self.pool, self.sp]
        self.dma_bufs = []
        for e in self.engs:
            e.epoch()

    def new_sem(self, name):
        self.nsem += 1
        return self.ctx.enter_context(self.nc.semaphore(f"{name}_{self.nsem}"))

    def sb(self, name, shape, dt):
        return self.ctx.enter_context(self.nc.sbuf_tensor(name, shape, dt))

    @staticmethod
    def _deps(reads, writes):
        deps = []
        for b in reads:
            deps.append(b.w)
        for b in writes:
            deps.append(b.w)
            deps.extend(b.r.values())
        return deps

    @staticmethod
    def _note(tok, reads, writes):
        for b in reads:
            b.r[id(tok[0])] = tok
        for b in writes:
            b.w = tok
            b.r = {}

    def op(self, eng, fns, reads=(), writes=()):
        eng.wait(self._deps(reads, writes))
        if callable(fns):
            fns = [fns]
        ins = None
        for f in fns:
            ins = f()
        tok = eng.mark(ins)
        self._note(tok, reads, writes)
        for b in writes:
            b._wdma = False
        return tok

    def dma(self, eng, out, in_, sbuf, reads=(), writes=(), disjoint=False):
        if sbuf.dsem is None:
            sbuf.dsem = self.new_sem("d_" + sbuf.name)
            self.dma_bufs.append(sbuf)
        deps = []
        for b in reads:
            deps.append(b.w)
        for b in writes:
            if b.w is not None and not (disjoint and b.w[0] is sbuf.dsem and b is sbuf and b._wdma):
                deps.append(b.w)
            deps.extend(b.r.values())
        eng.wait(deps)
        sbuf.dcnt += 16
        eng.e.dma_start(out=out, in_=in_).then_inc(sbuf.dsem, 16)
        tok = (sbuf.dsem, sbuf.dcnt, None)
        self._note(tok, reads, writes)
        for b in writes:
            b._wdma = True
        return tok

    def barrier(self):
        toks = [(e.sem, e.cnt, None) for e in self.engs if e.cnt > 0]
        toks += [(b.dsem, b.dcnt, None) for b in self.dma_bufs]
        for e in self.engs:
            e.wait(toks)


def build_program(nseq=SEQ_PER_CORE, layers=(0, 1, 2, 3)):
    nc = bass.Bass("TRN2", target_bir_lowering=False)
    dr = lambda n, s, k: nc.dram_tensor(n, list(s), F32, kind=k).ap()
    xin = dr("xin", [nseq, S, D], "ExternalInput")
    yout = dr("yout", [nseq, S, D], "ExternalOutput")
    scrA = dr("scrA", [nseq, S, D], "Internal")
    scrB = dr("scrB", [nseq, S, D], "Internal")
    gpre_d = dr("gpre_b", [4, 128, D], "ExternalInput")
    gpost_d = dr("gpost_b", [4, 128, D], "ExternalInput")
    awin_d = dr("attn_w_in", [2, D, 2560], "ExternalInput")
    awout_d = dr("attn_w_out", [2, D, D], "ExternalInput")
    gqk_d = dr("gqk_b", [2, 4, 128, 128], "ExternalInput")
    cwin_d = dr("conv_w_in", [2, D, 3072], "ExternalInput")
    cwout_d = dr("conv_w_out", [2, D, D], "ExternalInput")
    cvec_d = dr("conv_vec", [2, 128, 8, 34], "ExternalInput")
    cs_d = dr("cs_t", [S, 2, 128], "ExternalInput")
    idn_d = dr("ident", [128, 128], "ExternalInput")

    ctx = ExitStack()
    with ctx:
        kb = KB(nc, ctx)
        pe, act, dve, pool, sp = kb.pe, kb.act, kb.dve, kb.pool, kb.sp
        PE, ACT, DVE, POOL = nc.tensor, nc.scalar, nc.vector, nc.gpsimd

        ps = ctx.enter_context(nc.psum_tensor("ps", [128, 4096], F32))
        bank = [Buf(f"bank{i}") for i in range(8)]
        psf = lambda b, n=1: ps[:, b * 512:(b + n) * 512]
        psb = lambda b: ps[:, b * 512:(b + 1) * 512].bitcast(BF16)

        ident = kb.sb("ident_sb", [128, 128], BF16); ident_b = Buf("ident")
        ones = kb.sb("ones", [128, 128], BF16); ones_b = Buf("ones")
        mhalf = kb.sb("mhalf", [128, 16], F32); mhalf_b = Buf("mhalf")
        gpre = kb.sb("gpre", [128, D], F32); gpre_b = Buf("gpre")
        gpost = kb.sb("gpost", [128, D], F32); gpost_b = Buf("gpost")
        WA_b = Buf("WA")
        WZ = kb.sb("WZ", [128, 8, 1024], BF16); WZ_b = Buf("WZ")
        WO = kb.sb("WO", [128, 8, 1024], BF16); WO_b = Buf("WO")
        NX = 3
        xt = [kb.sb(f"xt{i}", [128, D], F32) for i in range(NX)]; xt_b = [Buf(f"xt{i}") for i in range(NX)]
        hb = [kb.sb(f"hb{i}", [128, D], BF16) for i in range(2)]; hb_b = [Buf(f"hb{i}") for i in range(2)]
        sq = kb.sb("sq", [128, D], BF16); sq_b = Buf("sq")
        NST = 16
        st = [kb.sb(f"st{i}", [128, 8], F32) for i in range(NST)]; st_b = [Buf(f"st{i}") for i in range(NST)]
        tmp = kb.sb("tmp", [128, D], F32); tmp_b = Buf("tmp")
        yt = [kb.sb(f"yt{i}", [128, D], F32) for i in range(2)]; yt_b = [Buf(f"yt{i}") for i in range(2)]
        GT = kb.sb("GT", [128, 8, 512], BF16); GT_b = [Buf(f"GT{i}") for i in range(8)]
        SZ = kb.sb("SZ", [128, 8, 512], BF16); SZ_b = [Buf(f"SZ{i}") for i in range(8)]

        kb.dma(pool, ident[:], idn_d, ident_b, writes=[ident_b])
        kb.op(dve, lambda: DVE.memset(ones[:], 1.0), writes=[ones_b])
        kb.op(dve, lambda: DVE.memset(mhalf[:], -0.5), writes=[mhalf_b])

        cnt = {"x": 0, "hb": 0, "st": 0, "y": 0}

        def dram_bufs(tag):
            return [[Buf(f"{tag}_{s}_{t}") for t in range(NT)] for s in range(nseq)]
        dbufs = {id(xin): dram_bufs("xin"), id(scrA): dram_bufs("scrA"), id(scrB): dram_bufs("scrB"),
                 id(yout): dram_bufs("yout")}

        def load_weights(dst, dst_b, src, col0, ncol, dcol0=0):
            for kc in range(8):
                kb.dma(pool, dst[:, kc, dcol0:dcol0 + ncol], src[kc * 128:(kc + 1) * 128, col0:col0 + ncol],
                       dst_b, writes=[dst_b], disjoint=True)

        def rstd_from_ss(stt, stb, c0, n, inv_n):
            v = stt[:, c0:c0 + n]
            kb.op(act, lambda: ACT.activation(out=v, in_=v, func=AF.Ln, scale=inv_n, bias=EPS),
                  reads=[stb], writes=[stb])
            kb.op(act, lambda: ACT.activation(out=v, in_=v, func=AF.Exp, scale=-0.5),
                  reads=[stb], writes=[stb])

        def prenorm_a(src, s, t):
            i = cnt["x"] % NX; cnt["x"] += 1
            X, Xb = xt[i], xt_b[i]
            kb.dma(sp, X[:], src[s, t * 128:(t + 1) * 128, :], Xb, reads=[dbufs[id(src)][s][t]], writes=[Xb])
            j = cnt["st"] % NST; cnt["st"] += 1
            T, Tb = st[j], st_b[j]
            kb.op(act, lambda: ACT.activation(out=sq[:], in_=X[:], func=AF.Square, accum_out=T[:, 0:1]),
                  reads=[Xb], writes=[sq_b, Tb])
            rstd_from_ss(T, Tb, 0, 1, 1.0 / D)
            h = cnt["hb"] % 2; cnt["hb"] += 1
            H, Hb = hb[h], hb_b[h]
            kb.op(dve, lambda: DVE.scalar_tensor_tensor(out=H[:], in0=X[:], scalar=T[:, 0:1], in1=gpre[:],
                                                        op0=ALU.mult, op1=ALU.mult),
                  reads=[Xb, Tb, gpre_b], writes=[Hb])
            return H, Hb

        def prenorm_b(H, Hb, dstT, dst_bufs, col, pbank):
            pv = psb(pbank)
            kb.op(pe, [(lambda c=c: PE.transpose(pv[:, c * 128:(c + 1) * 128], H[:, c * 128:(c + 1) * 128], ident[:]))
                       for c in range(8)],
                  reads=[Hb, ident_b], writes=[bank[pbank]])
            kb.op(dve, lambda: DVE.tensor_copy(out=dstT[:, :, col:col + 128],
                                               in_=pv.rearrange("p (c t) -> p c t", c=8)),
                  reads=[bank[pbank]], writes=dst_bufs)

        def prenorm_tile(src, s, t, dstT, dst_bufs, col, pbank):
            H, Hb = prenorm_a(src, s, t)
            prenorm_b(H, Hb, dstT, dst_bufs, col, pbank)

        def out_tile_a(tt, pb0):
            for hf in range(2):
                kb.op(pe, [(lambda kc=kc: PE.matmul(psf(pb0 + hf), lhsT=GT[:, kc, tt * 128:(tt + 1) * 128],
                                                    rhs=WO[:, kc, hf * 512:(hf + 1) * 512],
                                                    start=(kc == 0), stop=(kc == 7))) for kc in range(8)],
                      reads=GT_b + [WO_b], writes=[bank[pb0 + hf]])

        def out_tile_b(src, dst, s, t, pb0):
            mps = psf(pb0, 2)
            j = cnt["st"] % NST; cnt["st"] += 1
            T, Tb = st[j], st_b[j]
            kb.op(act, lambda: ACT.activation(out=sq[:], in_=mps, func=AF.Square, accum_out=T[:, 0:1]),
                  reads=[bank[pb0], bank[pb0 + 1]], writes=[sq_b, Tb])
            rstd_from_ss(T, Tb, 0, 1, 1.0 / D)
            kb.op(dve, lambda: DVE.scalar_tensor_tensor(out=tmp[:], in0=mps, scalar=T[:, 0:1], in1=gpost[:],
                                                        op0=ALU.mult, op1=ALU.mult),
                  reads=[bank[pb0], bank[pb0 + 1], Tb, gpost_b], writes=[tmp_b])
            i = cnt["x"] % NX; cnt["x"] += 1
            X, Xb = xt[i], xt_b[i]
            kb.dma(sp, X[:], src[s, t * 128:(t + 1) * 128, :], Xb, reads=[dbufs[id(src)][s][t]], writes=[Xb])
            k = cnt["y"] % 2; cnt["y"] += 1
            Y, Yb = yt[k], yt_b[k]
            kb.op(dve, lambda: DVE.tensor_tensor(out=Y[:], in0=tmp[:], in1=X[:], op=ALU.add),
                  reads=[tmp_b, Xb], writes=[Yb])
            kb.dma(sp, dst[s, t * 128:(t + 1) * 128, :], Y[:], Yb, reads=[Yb], writes=[dbufs[id(dst)][s][t]])

        def out_tile(src, dst, s, t, tt, pb0):
            out_tile_a(tt, pb0)
            out_tile_b(src, dst, s, t, pb0)

        def attn_layer(L, j, src, dst, lctx):
            lsb = lambda n, shp, dt: lctx.enter_context(nc.sbuf_tensor(f"{n}_L{L}", shp, dt))
            WA = lsb("WA", [128, 8, 1536], BF16)
            KT = lsb("KT", [128, NKV, S], BF16); KT_b = [Buf(f"KT{t}") for t in range(NT)]
            V = lsb("V", [128, NT, 256], BF16); V_b = [Buf(f"V{t}") for t in range(NT)]
            gqk = lsb("gqk", [128, 4, 128], F32); gqk_b = Buf("gqk")
            HT = lsb("HT", [128, 8, 512], BF16); HT_b = [Buf(f"HT{i}") for i in range(4)]
            QT2 = [lsb(f"QT{i}", [128, 8, 512], BF16) for i in range(2)]
            QT2_b = [[Buf(f"QT{i}_{k}") for k in range(4)] for i in range(2)]
            QW = 512
            qf = lsb("qf", [128, QW], F32); qf_b = Buf("qf")
            t1 = lsb("t1", [128, QW], F32); t1_b = Buf("t1")
            t2 = lsb("t2", [128, QW], F32); t2_b = Buf("t2")
            qr = lsb("qr", [128, QW], BF16); qr_b = Buf("qr")
            NCS = 3
            cs = [lsb(f"cs{i}", [128, 2, 128], F32) for i in range(NCS)]; cs_b = [Buf(f"cs{i}") for i in range(NCS)]
            csq = [lsb(f"csq{i}", [128, 2, 128], F32) for i in range(2)]; csq_b = [Buf(f"csq{i}") for i in range(2)]
            qcnt = [0]
            kf = lsb("kf", [128, 2, 256], F32); kf_b = [Buf(f"kf{i}") for i in range(2)]
            k1 = lsb("k1", [128, 2, 256], F32); k1_b = [Buf(f"k1{i}") for i in range(2)]
            k2 = lsb("k2", [128, 2, 256], F32); k2_b = [Buf(f"k2{i}") for i in range(2)]
            kr = lsb("kr", [128, 2, 256], BF16); kr_b = [Buf(f"kr{i}") for i in range(2)]
            NPT = 4
            PT = [lsb(f"PT{i}", [128, 1024], BF16) for i in range(NPT)]; PT_b = [Buf(f"PT{i}") for i in range(NPT)]
            ld = lsb("ld", [128, 512], F32); ld_b = Buf("ld")
            on = lsb("on", [128, 512], F32); on_b = Buf("on")
            ze = lsb("ze", [128, 512], F32); ze_b = Buf("ze")
            NPS = 4
            PS2 = [lsb(f"PS{i}", [128, 512], BF16) for i in range(NPS)]; PS2_b = [Buf(f"PS{i}") for i in range(NPS)]

            load_weights(WA, WA_b, awin_d[j], 0, 1536)
            load_weights(WZ, WZ_b, awin_d[j], 1536, 1024)
            load_weights(WO, WO_b, awout_d[j], 0, 1024)
            kb.dma(sp, gpre[:], gpre_d[L], gpre_b, writes=[gpre_b])
            kb.dma(sp, gpost[:], gpost_d[L], gpost_b, writes=[gpost_b])
            kb.dma(sp, gqk[:], gqk_d[j].rearrange("a p d -> p a d"), gqk_b, writes=[gqk_b])
            ccnt = [0]

            def rope_tables(t, which):
                i = ccnt[0] % NCS; ccnt[0] += 1
                C, Cb = cs[i], cs_b[i]
                kb.dma(sp, C[:, 0:2, :], cs_d[t * 128:(t + 1) * 128, :, :], Cb, writes=[Cb])
                kb.op(pool, lambda: POOL.tensor_tensor(out=C[:, 2:4, :], in0=C[:, 0:2, :],
                                                       in1=gqk[:, 2 * which:2 * which + 2, :], op=ALU.mult),
                      reads=[Cb, gqk_b], writes=[Cb])
                return C, Cb

            def norm_rope(src_ps, src_banks, nh, C, Cb):
                W = nh * 128
                j2 = cnt["st"] % NST; cnt["st"] += 1
                T, Tb = st[j2], st_b[j2]
                kb.op(act, lambda: ACT.activation(out=sq[:, 0:W], in_=src_ps, func=AF.Square),
                      reads=src_banks, writes=[sq_b])
                kb.op(act, lambda: ACT.activation(out=qf[:, 0:W], in_=src_ps, func=AF.Copy),
                      reads=src_banks, writes=[qf_b])
                kb.op(dve, lambda: DVE.tensor_reduce(out=T[:, 0:nh], in_=sq[:, 0:W].rearrange("p (h d) -> p h d", h=nh),
                                                     op=ALU.add, axis=AX.X),
                      reads=[sq_b], writes=[Tb])
                rstd_from_ss(T, Tb, 0, nh, 1.0 / HD)
                q3 = qf[:, 0:W].rearrange("p (h d) -> p h d", h=nh)
                kb.op(dve, lambda: DVE.tensor_tensor(out=q3, in0=q3, in1=T[:, 0:nh].unsqueeze(2).to_broadcast([128, nh, 128]),
                                                     op=ALU.mult),
                      reads=[qf_b, Tb], writes=[qf_b])
                kb.op(dve, lambda: DVE.tensor_tensor(out=t1[:, 0:W].rearrange("p (h d) -> p h d", h=nh), in0=q3,
                                                     in1=C[:, 2:3, :].to_broadcast([128, nh, 128]), op=ALU.mult),
                      reads=[qf_b, Cb], writes=[t1_b])
                q4 = qf[:, 0:W].rearrange("p (h r f e) -> p h r f e", h=nh, r=2, f=2)
                o4 = t2[:, 0:W].rearrange("p (h r f e) -> p h r f e", h=nh, r=2, f=2)
                s4 = C[:, 3, :].rearrange("p (r f e) -> p r f e", r=2, f=2)
                fns = []
                for f in range(2):
                    fns.append(lambda f=f: POOL.tensor_tensor(
                        out=o4[:, :, :, f, :], in0=q4[:, :, :, 1 - f, :],
                        in1=s4[:, :, f, :].unsqueeze(1).to_broadcast([128, nh, 2, 32]), op=ALU.mult))
                kb.op(pool, fns, reads=[qf_b, Cb], writes=[t2_b])
                kb.op(dve, lambda: DVE.tensor_tensor(out=qr[:, 0:W], in0=t1[:, 0:W], in1=t2[:, 0:W], op=ALU.add),
                      reads=[t1_b, t2_b], writes=[qr_b])

            scale = float(HD) ** -0.5
            for s in range(nseq):
                PB = 6

                def prologue_stages(b, stride=17):
                    nb = b % 2
                    stages = []
                    for tt in range(4):
                        t = 4 * b + tt
                        o = tt * stride
                        stv = {}

                        def L(t=t, stv=stv):
                            i = cnt["x"] % NX; cnt["x"] += 1
                            stv["X"] = (xt[i], xt_b[i])
                            kb.dma(sp, xt[i][:], src[s, t * 128:(t + 1) * 128, :], xt_b[i],
                                   reads=[dbufs[id(src)][s][t]], writes=[xt_b[i]])
                            ci = qcnt[0] % 2; qcnt[0] += 1
                            stv["C"] = (csq[ci], csq_b[ci])
                            kb.dma(sp, csq[ci][:], cs_d[t * 128:(t + 1) * 128, :, :], csq_b[ci], writes=[csq_b[ci]])

                        def N1(stv=stv):
                            X, Xb = stv["X"]
                            j2 = cnt["st"] % NST; cnt["st"] += 1
                            T, Tb = st[j2], st_b[j2]
                            stv["T"] = (T, Tb)
                            kb.op(act, lambda: ACT.activation(out=sq[:], in_=X[:], func=AF.Square, accum_out=T[:, 0:1]),
                                  reads=[Xb], writes=[sq_b, Tb])

                        def N1b(stv=stv):
                            T, Tb = stv["T"]
                            rstd_from_ss(T, Tb, 0, 1, 1.0 / D)

                        def N2(stv=stv):
                            X, Xb = stv["X"]; T, Tb = stv["T"]; C, Cb = stv["C"]
                            h = cnt["hb"] % 2; cnt["hb"] += 1
                            H, Hb = hb[h], hb_b[h]
                            stv["H"] = (H, Hb)
                            kb.op(dve, lambda: DVE.scalar_tensor_tensor(out=H[:], in0=X[:], scalar=T[:, 0:1], in1=gpre[:],
                                                                        op0=ALU.mult, op1=ALU.mult),
                                  reads=[Xb, Tb, gpre_b], writes=[Hb])
                            kb.op(pool, lambda: POOL.tensor_tensor(out=C[:], in0=C[:], in1=gqk[:, 0:2, :], op=ALU.mult),
                                  reads=[Cb, gqk_b], writes=[Cb])

                        def T1(stv=stv):
                            H, Hb = stv["H"]
                            pv = psb(PB)
                            kb.op(pe, [(lambda c=c: PE.transpose(pv[:, c * 128:(c + 1) * 128], H[:, c * 128:(c + 1) * 128], ident[:]))
                                       for c in range(8)], reads=[Hb, ident_b], writes=[bank[PB]])

                        def C1(tt=tt):
                            pv = psb(PB)
                            kb.op(dve, lambda: DVE.tensor_copy(out=HT[:, :, tt * 128:(tt + 1) * 128],
                                                               in_=pv.rearrange("p (c t) -> p c t", c=8)),
                                  reads=[bank[PB]], writes=[HT_b[tt]])
                        stages += [(o + 0, L), (o + 4, N1), (o + 5, N1b), (o + 6, N2), (o + 7, T1), (o + 8, C1)]
                        for hf in range(2):
                            oo = o + 9 + 5 * hf

                            def Q1(tt=tt, hf=hf):
                                kb.op(pe, [(lambda kc=kc: PE.matmul(psf(PB + 1), lhsT=HT[:, kc, tt * 128:(tt + 1) * 128],
                                                                    rhs=WA[:, kc, hf * 512:(hf + 1) * 512],
                                                                    start=(kc == 0), stop=(kc == 7))) for kc in range(8)],
                                      reads=[HT_b[tt], WA_b], writes=[bank[PB + 1]])

                            def R1a(stv=stv):
                                j2 = cnt["st"] % NST; cnt["st"] += 1
                                T, Tb = st[j2], st_b[j2]
                                stv["T2"] = (T, Tb)
                                src_ps = psf(PB + 1)
                                kb.op(act, [(lambda h=h: ACT.activation(out=sq[:, h * 128:(h + 1) * 128], in_=src_ps[:, h * 128:(h + 1) * 128],
                                                                        func=AF.Square, accum_out=T[:, h:h + 1])) for h in range(2)],
                                      reads=[bank[PB + 1]], writes=[sq_b, Tb])

                            def R1b(stv=stv):
                                T, Tb = stv["T2"]
                                src_ps = psf(PB + 1)
                                kb.op(act, [(lambda h=h: ACT.activation(out=sq[:, h * 128:(h + 1) * 128], in_=src_ps[:, h * 128:(h + 1) * 128],
                                                                        func=AF.Square, accum_out=T[:, h:h + 1])) for h in range(2, 4)],
                                      reads=[bank[PB + 1]], writes=[sq_b, Tb])

                            def R1c(stv=stv):
                                kb.op(act, lambda: ACT.activation(out=qf[:], in_=psf(PB + 1), func=AF.Copy),
                                      reads=[bank[PB + 1]], writes=[qf_b])

                            def R1d(stv=stv):
                                T, Tb = stv["T2"]
                                rstd_from_ss(T, Tb, 0, 4, 1.0 / HD)

                            def R2(stv=stv):
                                T, Tb = stv["T2"]; C, Cb = stv["C"]
                                q3 = qf[:].rearrange("p (h d) -> p h d", h=4)
                                kb.op(dve, lambda: DVE.tensor_tensor(out=q3, in0=q3, in1=T[:, 0:4].unsqueeze(2).to_broadcast([128, 4, 128]),
                                                                     op=ALU.mult), reads=[qf_b, Tb], writes=[qf_b])
                                kb.op(dve, lambda: DVE.tensor_tensor(out=t1[:].rearrange("p (h d) -> p h d", h=4), in0=q3,
                                                                     in1=C[:, 0:1, :].to_broadcast([128, 4, 128]), op=ALU.mult),
                                      reads=[qf_b, Cb], writes=[t1_b])

                            def R2b(stv=stv):
                                C, Cb = stv["C"]
                                q4 = qf[:].rearrange("p (h r f e) -> p h r f e", h=4, r=2, f=2)
                                o4 = t2[:].rearrange("p (h r f e) -> p h r f e", h=4, r=2, f=2)
                                s4 = C[:, 1, :].rearrange("p (r f e) -> p r f e", r=2, f=2)
                                kb.op(pool, [(lambda f=f: POOL.tensor_tensor(
                                    out=o4[:, :, :, f, :], in0=q4[:, :, :, 1 - f, :],
                                    in1=s4[:, :, f, :].unsqueeze(1).to_broadcast([128, 4, 2, 32]), op=ALU.mult)) for f in range(2)],
                                      reads=[qf_b, Cb], writes=[t2_b])

                            def R3():
                                kb.op(dve, lambda: DVE.tensor_tensor(out=qr[:], in0=t1[:], in1=t2[:], op=ALU.add),
                                      reads=[t1_b, t2_b], writes=[qr_b])

                            def T2():
                                pv = psb(PB)
                                kb.op(pe, [(lambda h=h: PE.transpose(pv[:, h * 128:(h + 1) * 128], qr[:, h * 128:(h + 1) * 128], ident[:]))
                                           for h in range(4)], reads=[qr_b, ident_b], writes=[bank[PB]])

                            def C2(tt=tt, hf=hf):
                                pv = psb(PB)
                                kb.op(dve, lambda: DVE.tensor_copy(out=QT2[nb][:, 4 * hf:4 * hf + 4, tt * 128:(tt + 1) * 128],
                                                                   in_=pv[:, 0:512].rearrange("p (h t) -> p h t", h=4)),
                                      reads=[bank[PB]], writes=[QT2_b[nb][tt]])
                            stages += [(oo, Q1), (oo + 2, R1a), (oo + 3, R1b), (oo + 4, R1c), (oo + 5, R1d), (oo + 6, R2), (oo + 6, R2b),
                                       (oo + 7, R3), (oo + 8, T2), (oo + 9, C2)]
                    stages.sort(key=lambda x: x[0])
                    return stages

                def z_stages(fc):
                    zb = PB + (fc % 2)

                    def Z1():
                        kb.op(pe, [(lambda kc=kc: PE.matmul(psf(zb), lhsT=WZ[:, kc, fc * 128:(fc + 1) * 128], rhs=HT[:, kc, :],
                                                            start=(kc == 0), stop=(kc == 7))) for kc in range(8)],
                              reads=HT_b + [WZ_b], writes=[bank[zb]])

                    def Z2a():
                        kb.op(act, lambda: ACT.activation(out=ze[:], in_=psf(zb), func=AF.Exp, scale=-1.0),
                              reads=[bank[zb]], writes=[ze_b])

                    def Z2b():
                        kb.op(act, lambda: ACT.activation(out=ze[:], in_=ze[:], func=AF.Ln, bias=1.0), reads=[ze_b], writes=[ze_b])

                    def Z2c():
                        kb.op(act, lambda: ACT.activation(out=ze[:], in_=ze[:], func=AF.Exp, scale=-1.0), reads=[ze_b], writes=[ze_b])

                    def Z3():
                        kb.op(dve, lambda: DVE.tensor_tensor(out=SZ[:, fc, :], in0=psf(zb), in1=ze[:], op=ALU.mult),
                              reads=[bank[zb], ze_b], writes=[SZ_b[fc]])
                    return [(0, Z1), (2, Z2a), (3, Z2b), (4, Z2c), (5, Z3)]

                def epilogue_stages(b, tt):
                    t = 4 * b + tt
                    stv = {}

                    def Ea():
                        i = cnt["x"] % NX; cnt["x"] += 1
                        stv["X"] = (xt[i], xt_b[i])
                        kb.dma(sp, xt[i][:], src[s, t * 128:(t + 1) * 128, :], xt_b[i],
                               reads=[dbufs[id(src)][s][t]], writes=[xt_b[i]])
                        out_tile_a(tt, PB)

                    def Eb1():
                        j2 = cnt["st"] % NST; cnt["st"] += 1
                        T, Tb = st[j2], st_b[j2]
                        stv["T"] = (T, Tb)
                        kb.op(act, lambda: ACT.activation(out=sq[:], in_=psf(PB, 2), func=AF.Square, accum_out=T[:, 0:1]),
                              reads=[bank[PB], bank[PB + 1]], writes=[sq_b, Tb])

                    def Eb1b():
                        T, Tb = stv["T"]
                        rstd_from_ss(T, Tb, 0, 1, 1.0 / D)

                    def Eb2():
                        T, Tb = stv["T"]
                        kb.op(dve, lambda: DVE.scalar_tensor_tensor(out=tmp[:], in0=psf(PB, 2), scalar=T[:, 0:1], in1=gpost[:],
                                                                    op0=ALU.mult, op1=ALU.mult),
                              reads=[bank[PB], bank[PB + 1], Tb, gpost_b], writes=[tmp_b])

                    def Eb3():
                        X, Xb = stv["X"]
                        k = cnt["y"] % 2; cnt["y"] += 1
                        Y, Yb = yt[k], yt_b[k]
                        kb.op(dve, lambda: DVE.tensor_tensor(out=Y[:], in0=tmp[:], in1=X[:], op=ALU.add),
                              reads=[tmp_b, Xb], writes=[Yb])
                        kb.dma(sp, dst[s, t * 128:(t + 1) * 128, :], Y[:], Yb, reads=[Yb], writes=[dbufs[id(dst)][s][t]])
                    return [(0, Ea), (3, Eb1), (4, Eb1b), (5, Eb2), (6, Eb3)]

                items = [(h, g) for h in range(NH) for g in range(16)]
                cur = {}

                def emit_S(b, n):
                    h, g = items[n]
                    sl = n % 2
                    kvh = h // 4
                    QT, QT_b = QT2[b % 2], QT2_b[b % 2]
                    kb.op(pe, [(lambda i=i: PE.matmul(psf(2 * sl + i), lhsT=KT[:, kvh, (2 * g + i) * 128:(2 * g + i + 1) * 128],
                                                      rhs=QT[:, h, :], start=True, stop=True)) for i in range(2)],
                          reads=QT_b + [KT_b[2 * g], KT_b[2 * g + 1]], writes=[bank[2 * sl], bank[2 * sl + 1]])

                def emit_den(n):
                    h, g = items[n]
                    Q, Qb = PS2[n % NPS], PS2_b[n % NPS]
                    kb.op(pe, lambda: PE.matmul(psf(5), lhsT=ones[:], rhs=Q[:], start=(g == 0), stop=(g == 15)),
                          reads=[Qb, ones_b], writes=[bank[5]])
                    if g == 15:
                        kb.op(dve, lambda: DVE.tensor_copy(out=on[:], in_=psf(4)), reads=[bank[4]], writes=[on_b])
                        kb.op(act, lambda: ACT.activation(out=ld[:], in_=psf(5), func=AF.Ln), reads=[bank[5]], writes=[ld_b])
                        def fin_exp():
                            kb.op(act, lambda: ACT.activation(out=ld[:], in_=ld[:], func=AF.Exp, scale=-1.0),
                                  reads=[ld_b], writes=[ld_b])

                        def fin_mul():
                            kb.op(dve, lambda: DVE.tensor_tensor(out=on[:], in0=on[:], in1=ld[:], op=ALU.mult),
                                  reads=[on_b, ld_b], writes=[on_b])

                        def gt_write(h=h):
                            kb.op(pool, lambda: POOL.tensor_tensor(out=GT[:, h, :], in0=on[:], in1=SZ[:, h, :], op=ALU.mult),
                                  reads=[on_b, SZ_b[h]], writes=[GT_b[h]])
                        pos = cur["n"]
                        cur["at"](pos + 1, fin_exp)
                        cur["at"](pos + 2, fin_mul)
                        if h == 0 and cur["b"] > 0:
                            cur["at"](max(27, pos + 2), gt_write)
                        else:
                            cur["at"](pos + 2, gt_write)

                def emit_rest(b, n):
                    h, g = items[n]
                    sl = n % 2
                    kvh = h // 4
                    P, Pb = PT[n % NPT], PT_b[n % NPT]
                    Q, Qb = PS2[n % NPS], PS2_b[n % NPS]
                    kb.op(act, lambda: ACT.activation(out=P[:], in_=psf(2 * sl, 2), func=AF.Exp, scale=scale),
                          reads=[bank[2 * sl], bank[2 * sl + 1]], writes=[Pb])
                    kb.op(dve, lambda: DVE.tensor_tensor(out=Q[:], in0=P[:, 0:512], in1=P[:, 512:1024], op=ALU.add),
                          reads=[Pb], writes=[Qb])
                    cur["n"] = n
                    if g == 0:
                        if n >= 2:
                            emit_den(n - 2)
                        if n >= 1:
                            emit_den(n - 1)
                    elif g >= 2:
                        emit_den(n - 2)
                    fns = []
                    for i in range(2):
                        kc = 2 * g + i
                        fns.append(lambda i=i, kc=kc: PE.matmul(psf(4), lhsT=V[:, kc, kvh * 128:(kvh + 1) * 128],
                                                                rhs=P[:, i * 512:(i + 1) * 512],
                                                                start=(kc == 0), stop=(kc == NT - 1)))
                    kb.op(pe, fns, reads=[Pb, V_b[2 * g], V_b[2 * g + 1]], writes=[bank[4]])

                stA = {}

                def aL(t):
                    i = cnt["x"] % NX; cnt["x"] += 1
                    stA[(t, "X")] = (xt[i], xt_b[i])
                    kb.dma(sp, xt[i][:], src[s, t * 128:(t + 1) * 128, :], xt_b[i],
                           reads=[dbufs[id(src)][s][t]], writes=[xt_b[i]])

                def aN1(t):
                    X, Xb = stA[(t, "X")]
                    j2 = cnt["st"] % NST; cnt["st"] += 1
                    T, Tb = st[j2], st_b[j2]
                    stA[(t, "T")] = (T, Tb)
                    kb.op(act, lambda: ACT.activation(out=sq[:], in_=X[:], func=AF.Square, accum_out=T[:, 0:1]),
                          reads=[Xb], writes=[sq_b, Tb])
                    rstd_from_ss(T, Tb, 0, 1, 1.0 / D)

                def aN2(t):
                    X, Xb = stA.pop((t, "X")); T, Tb = stA.pop((t, "T"))
                    h = cnt["hb"] % 2; cnt["hb"] += 1
                    H, Hb = hb[h], hb_b[h]
                    stA[(t, "H")] = (H, Hb)
                    kb.op(dve, lambda: DVE.scalar_tensor_tensor(out=H[:], in0=X[:], scalar=T[:, 0:1], in1=gpre[:],
                                                                op0=ALU.mult, op1=ALU.mult),
                          reads=[Xb, Tb, gpre_b], writes=[Hb])

                def aT1(t):
                    H, Hb = stA.pop((t, "H"))
                    pv = psb(t % 2)
                    kb.op(pe, [(lambda c=c: PE.transpose(pv[:, c * 128:(c + 1) * 128], H[:, c * 128:(c + 1) * 128], ident[:]))
                               for c in range(8)], reads=[Hb, ident_b], writes=[bank[t % 2]])

                def aC1(t):
                    par = t % 2
                    kb.op(dve, lambda: DVE.tensor_copy(out=HT[:, :, par * 128:(par + 1) * 128],
                                                       in_=psb(par).rearrange("p (c t) -> p c t", c=8)),
                          reads=[bank[par]], writes=[HT_b[par]])

                def aK1(t):
                    par = t % 2
                    kb.op(pe, [(lambda kc=kc: PE.matmul(psf(2 + par), lhsT=HT[:, kc, par * 128:(par + 1) * 128],
                                                        rhs=WA[:, kc, 1024:1536], start=(kc == 0), stop=(kc == 7)))
                               for kc in range(8)], reads=[HT_b[par], WA_b], writes=[bank[2 + par]])
                    ci = ccnt[0] % NCS; ccnt[0] += 1
                    stA[(t, "C")] = (cs[ci], cs_b[ci])
                    kb.dma(sp, cs[ci][:], cs_d[t * 128:(t + 1) * 128, :, :], cs_b[ci], writes=[cs_b[ci]])

                def aR1(t):
                    par = t % 2
                    kps = psf(2 + par)
                    C, Cb = stA[(t, "C")]
                    kb.op(pool, lambda: POOL.tensor_tensor(out=C[:], in0=C[:], in1=gqk[:, 2:4, :], op=ALU.mult),
                          reads=[Cb, gqk_b], writes=[Cb])
                    kb.op(act, lambda: ACT.activation(out=V[:, t, :], in_=kps[:, 256:512], func=AF.Copy),
                          reads=[bank[2 + par]], writes=[V_b[t]])
                    j2 = cnt["st"] % NST; cnt["st"] += 1
                    T, Tb = st[j2], st_b[j2]
                    stA[(t, "T2")] = (T, Tb)
                    kb.op(act, [(lambda h=h: ACT.activation(out=sq[:, h * 128:(h + 1) * 128], in_=kps[:, h * 128:(h + 1) * 128],
                                                            func=AF.Square, accum_out=T[:, h:h + 1])) for h in range(2)],
                          reads=[bank[2 + par]], writes=[sq_b, Tb])
                    kb.op(act, lambda: ACT.activation(out=kf[:, par, :], in_=kps[:, 0:256], func=AF.Copy),
                          reads=[bank[2 + par]], writes=[kf_b[par]])
                    rstd_from_ss(T, Tb, 0, 2, 1.0 / HD)

                def aR2(t):
                    par = t % 2
                    T, Tb = stA.pop((t, "T2")); C, Cb = stA.pop((t, "C"))
                    q3 = kf[:, par, :].rearrange("p (h d) -> p h d", h=2)
                    kb.op(dve, lambda: DVE.tensor_tensor(out=q3, in0=q3, in1=T[:, 0:2].unsqueeze(2).to_broadcast([128, 2, 128]),
                                                         op=ALU.mult), reads=[kf_b[par], Tb], writes=[kf_b[par]])
                    kb.op(dve, lambda: DVE.tensor_tensor(out=k1[:, par, :].rearrange("p (h d) -> p h d", h=2), in0=q3,
                                                         in1=C[:, 0:1, :].to_broadcast([128, 2, 128]), op=ALU.mult),
                          reads=[kf_b[par], Cb], writes=[k1_b[par]])
                    q4 = kf[:, par, :].rearrange("p (h r f e) -> p h r f e", h=2, r=2, f=2)
                    o4 = k2[:, par, :].rearrange("p (h r f e) -> p h r f e", h=2, r=2, f=2)
                    s4 = C[:, 1, :].rearrange("p (r f e) -> p r f e", r=2, f=2)
                    kb.op(pool, [(lambda f=f: POOL.tensor_tensor(
                        out=o4[:, :, :, f, :], in0=q4[:, :, :, 1 - f, :],
                        in1=s4[:, :, f, :].unsqueeze(1).to_broadcast([128, 2, 2, 32]), op=ALU.mult)) for f in range(2)],
                          reads=[kf_b[par], Cb], writes=[k2_b[par]])

                def aR3(t):
                    par = t % 2
                    kb.op(dve, lambda: DVE.tensor_tensor(out=kr[:, par, :], in0=k1[:, par, :], in1=k2[:, par, :], op=ALU.add),
                          reads=[k1_b[par], k2_b[par]], writes=[kr_b[par]])

                def aT2(t):
                    par = t % 2
                    pv = psb(4 + par)
                    kb.op(pe, [(lambda g=g: PE.transpose(pv[:, g * 128:(g + 1) * 128], kr[:, par, g * 128:(g + 1) * 128], ident[:]))
                               for g in range(2)], reads=[kr_b[par], ident_b], writes=[bank[4 + par]])

                def aC2(t):
                    par = t % 2
                    kb.op(dve, lambda: DVE.tensor_copy(out=KT[:, :, t * 128:(t + 1) * 128],
                                                       in_=psb(4 + par)[:, 0:256].rearrange("p (g t) -> p g t", g=2)),
                          reads=[bank[4 + par]], writes=[KT_b[t]])

                a_stages = [aL, aN1, aN2, aT1, aC1, aK1, aR1, aR2, aR3, aT2, aC2]
                NIT = NT + len(a_stages) - 1
                pro0 = prologue_stages(0)
                pi = 0
                for it in range(NIT):
                    for k in range(len(a_stages) - 1, -1, -1):
                        t = it - k
                        if 0 <= t < NT:
                            a_stages[k](t)
                while pi < len(pro0):
                    pro0[pi][1]()
                    pi += 1

                for fc in range(8):
                    for _, f in z_stages(fc):
                        f()

                for b in range(NB):
                    sched = {}

                    def at(pos, f):
                        sched.setdefault(min(pos, 127), []).append(f)
                    cur["b"] = b
                    cur["at"] = at
                    if b > 0:
                        for o, f in z_stages(7):
                            at(0 + o, f)
                        for tt in range(4):
                            for o, f in epilogue_stages(b - 1, tt):
                                at(6 + 6 * tt + o, f)
                    if b + 1 < NB:
                        for o, f in prologue_stages(b + 1):
                            at(31 + o, f)
                        for fc in range(7):
                            for o, f in z_stages(fc):
                                at(107 + 3 * fc + o, f)
                    emit_S(b, 0)
                    for n in range(len(items)):
                        if n + 1 < len(items):
                            emit_S(b, n + 1)
                        emit_rest(b, n)
                        if n == len(items) - 1:
                            cur["n"] = n + 1
                            emit_den(n - 1)
                            emit_den(n)
                        for f in sched.get(n, []):
                            f()
                for tt in range(4):
                    for _, f in epilogue_stages(NB - 1, tt):
                        f()

        def conv_layer(L, j, src, dst, lctx):
            lsb = lambda n, shp, dt: lctx.enter_context(nc.sbuf_tensor(f"{n}_L{L}", shp, dt))
            WA = lsb("WA", [128, 8, 2048], BF16)
            DG = [lsb(f"DG{i}", [128, KW, 128], BF16) for i in range(2)]; DG_b = [Buf(f"DG{i}") for i in range(2)]
            cv = lsb("cv", [128, 8, 34], F32); cv_b = Buf("cv")
            idf = lsb("idf", [128, 128], F32); idf_b = Buf("idf")
            Hh = [lsb(f"Hh{i}", [128, 8, HW], BF16) for i in range(2)]
            Hh_b = [[Buf(f"Hh{i}_{k}") for k in range(6)] for i in range(2)]
            U = lsb("U", [128, 8, HW], BF16); U_b = [Buf(f"U{c}") for c in range(8)]
            th = [lsb(f"th{i}", [128, HW], F32) for i in range(2)]; th_b = [Buf(f"th{i}") for i in range(2)]
            vv = lsb("vv", [128, 8, 512], F32); vv_b = [Buf(f"vv{c}") for c in range(8)]
            vb = [lsb(f"vb{i}", [128, 512], BF16) for i in range(2)]; vb_b = [Buf(f"vb{i}") for i in range(2)]
            v2 = [lsb(f"v2{i}", [128, 512], BF16) for i in range(2)]; v2_b = [Buf(f"v2{i}") for i in range(2)]
            mu = lsb("mu", [128, 512], F32); mu_b = Buf("mu")
            rs = lsb("rs", [128, 512], F32); rs_b = Buf("rs")
            w1 = [lsb(f"w1{i}", [128, 512], F32) for i in range(2)]; w1_b = [Buf(f"w1{i}") for i in range(2)]
            w2 = [lsb(f"w2{i}", [128, 512], F32) for i in range(2)]; w2_b = [Buf(f"w2{i}") for i in range(2)]
            dgd = nc.dram_tensor(f"dgd_L{L}", [8, 128, KW * 128], BF16, kind="Internal").ap()
            dgd_b = [Buf(f"dgd{c}") for c in range(8)]

            load_weights(WA, WA_b, cwin_d[j], 0, 2048)
            load_weights(WZ, WZ_b, cwin_d[j], 2048, 1024)
            load_weights(WO, WO_b, cwout_d[j], 0, 1024)
            kb.dma(sp, gpre[:], gpre_d[L], gpre_b, writes=[gpre_b])
            kb.dma(sp, gpost[:], gpost_d[L], gpost_b, writes=[gpost_b])
            kb.dma(sp, cv[:], cvec_d[j], cv_b, writes=[cv_b])
            kb.dma(sp, idf[:], idn_d, idf_b, writes=[idf_b])
            for c in range(8):
                G, Gb = DG[c % 2], DG_b[c % 2]
                kb.op(dve, [(lambda k=k: DVE.tensor_scalar(out=G[:, k, :], in0=idf[:], scalar1=cv[:, c, k:k + 1],
                                                           scalar2=0.5, op0=ALU.mult, op1=ALU.mult)) for k in range(KW)],
                      reads=[idf_b, cv_b], writes=[Gb])
                kb.dma(sp, dgd[c], G[:].rearrange("p k m -> p (k m)"), dgd_b[c], reads=[Gb], writes=[dgd_b[c]])
            gidx = [0]

            def dg_load(g):
                c = g % 8
                kb.dma(pool, DG[g % 2][:].rearrange("p k m -> p (k m)"), dgd[c], DG_b[g % 2],
                       reads=[dgd_b[c]], writes=[DG_b[g % 2]])

            GBL = [(s_, b_) for s_ in range(nseq) for b_ in range(NB)]
            NG = len(GBL)
            if True:
                def prep_stages(g):
                    s, b = GBL[g]
                    i = g % 2
                    hs = {}

                    def A(tt):
                        hs[tt] = prenorm_a(src, s, 4 * b + tt)

                    def B(tt):
                        H_, Hb_ = hs.pop(tt)
                        prenorm_b(H_, Hb_, Hh[i], [Hh_b[i][tt]], PAD + tt * 128, 6 + tt % 2)
                    return [[lambda: A(0), lambda: A(1)], [lambda: B(0), lambda: A(2)], [lambda: B(1), lambda: A(3)],
                            [lambda: B(2)], [lambda: B(3)]]

                def prep_block(g):
                    for st_ in prep_stages(g):
                        for f in st_:
                            f()

                def halo_copies(g):
                    i = g % 2
                    kb.op(pool, lambda: POOL.tensor_copy(out=Hh[i][:, :, PAD + 512:HW], in_=Hh[1 - i][:, :, PAD:2 * PAD]),
                          reads=[Hh_b[1 - i][0]], writes=[Hh_b[i][5]])
                    kb.op(pool, lambda: POOL.tensor_copy(out=Hh[1 - i][:, :, 0:PAD], in_=Hh[i][:, :, 512:512 + PAD]),
                          reads=[Hh_b[i][3]], writes=[Hh_b[1 - i][4]])

                def glu(g):
                    s, b = GBL[g]
                    H, Hb = Hh[g % 2], Hh_b[g % 2]
                    has = (b > 0, b + 1 < NB)

                    def pe_part(c):
                        ab, gb_ = 2 * (c % 2), 2 * (c % 2) + 1
                        hbk = 6 + (c % 2)

                        def mm(dst_ps, col0, kc, rhs):
                            return PE.matmul(dst_ps, lhsT=WA[:, kc, col0 + c * 128:col0 + (c + 1) * 128], rhs=rhs,
                                             start=(kc == 0), stop=(kc == 7))
                        kb.op(pe, [(lambda kc=kc: mm(psf(ab), 0, kc, H[:, kc, PAD:PAD + 512])) for kc in range(8)],
                              reads=Hb[0:4] + [WA_b], writes=[bank[ab]])
                        kb.op(pe, [(lambda kc=kc: mm(psf(gb_), 1024, kc, H[:, kc, PAD:PAD + 512])) for kc in range(8)],
                              reads=Hb[0:4] + [WA_b], writes=[bank[gb_]])
                        for side, hc0 in ((0, 0), (1, PAD + 512)):
                            if not has[side]:
                                continue
                            pa = psf(hbk)[:, side * 128:side * 128 + PAD]
                            pg = psf(hbk)[:, side * 128 + 32:side * 128 + 32 + PAD]
                            kb.op(pe, [(lambda kc=kc: mm(pa, 0, kc, H[:, kc, hc0:hc0 + PAD])) for kc in range(8)] +
                                      [(lambda kc=kc: mm(pg, 1024, kc, H[:, kc, hc0:hc0 + PAD])) for kc in range(8)],
                                  reads=[Hb[4 + side], WA_b], writes=[bank[hbk]])

                    def rest(c):
                        ab, gb_ = 2 * (c % 2), 2 * (c % 2) + 1
                        hbk = 6 + (c % 2)
                        T, Tb = th[c % 2], th_b[c % 2]
                        kb.op(act, lambda: ACT.activation(out=T[:, PAD:PAD + 512], in_=psf(gb_), func=AF.Tanh, scale=0.5),
                              reads=[bank[gb_]], writes=[Tb])
                        for side, hc0 in ((0, 0), (1, PAD + 512)):
                            if not has[side]:
                                kb.op(pool, lambda: POOL.memset(U[:, c, hc0:hc0 + PAD], 0.0), writes=[U_b[c]])
                                continue
                            pg = psf(hbk)[:, side * 128 + 32:side * 128 + 32 + PAD]
                            kb.op(act, lambda: ACT.activation(out=T[:, hc0:hc0 + PAD], in_=pg, func=AF.Tanh, scale=0.5),
                                  reads=[bank[hbk]], writes=[Tb])
                        kb.op(dve, lambda: DVE.scalar_tensor_tensor(out=U[:, c, PAD:PAD + 512], in0=T[:, PAD:PAD + 512], scalar=1.0,
                                                                    in1=psf(ab), op0=ALU.add, op1=ALU.mult),
                              reads=[Tb, bank[ab]], writes=[U_b[c]])
                        for side, hc0 in ((0, 0), (1, PAD + 512)):
                            if not has[side]:
                                continue
                            pa = psf(hbk)[:, side * 128:side * 128 + PAD]
                            kb.op(dve, lambda: DVE.scalar_tensor_tensor(
                                out=U[:, c, hc0:hc0 + PAD], in0=T[:, hc0:hc0 + PAD], scalar=1.0, in1=pa, op0=ALU.add, op1=ALU.mult),
                                  reads=[Tb, bank[hbk]], writes=[U_b[c]])
                    return pe_part, rest

                def glu_plain(g):
                    pe_part, rest = glu(g)
                    for c in range(9):
                        if c < 8:
                            pe_part(c)
                        if c >= 1:
                            rest(c - 1)

                def zgate(g):
                    H, Hb = Hh[g % 2], Hh_b[g % 2]
                    for fc in range(9):
                        if fc < 8:
                            zb = 6 + fc % 2
                            kb.op(pe, [(lambda kc=kc: PE.matmul(psf(zb), lhsT=WZ[:, kc, fc * 128:(fc + 1) * 128],
                                                                rhs=H[:, kc, PAD:PAD + 512], start=(kc == 0), stop=(kc == 7)))
                                       for kc in range(8)], reads=Hb[0:4] + [WZ_b], writes=[bank[zb]])
                        if fc >= 1:
                            f1 = fc - 1
                            kb.op(act, lambda: ACT.activation(out=SZ[:, f1, :], in_=psf(6 + f1 % 2), func=AF.Silu),
                                  reads=[bank[6 + f1 % 2]], writes=[SZ_b[f1]])

                def dwconv(first, side):
                    if first:
                        dg_load(gidx[0]); dg_load(gidx[0] + 1)
                    g0 = gidx[0]

                    def v_act(c):
                        cb = c % 4
                        kb.op(act, lambda: ACT.activation(out=vv[:, c, :], in_=psf(cb), func=AF.Identity, bias=cv[:, c, 31:32]),
                              reads=[bank[cb], cv_b], writes=[vv_b[c]])
                        kb.op(act, lambda: ACT.activation(out=v2[c % 2][:], in_=psf(cb), func=AF.Square, bias=cv[:, c, 31:32]),
                              reads=[bank[cb], cv_b], writes=[v2_b[c % 2]])
                        kb.op(act, lambda: ACT.activation(out=vb[c % 2][:], in_=psf(cb), func=AF.Identity, bias=cv[:, c, 31:32]),
                              reads=[bank[cb], cv_b], writes=[vb_b[c % 2]])

                    def v_st(c):
                        kb.op(pe, lambda: PE.matmul(psf(4), lhsT=ones[:], rhs=vb[c % 2][:], start=(c == 0), stop=(c == 7)),
                              reads=[vb_b[c % 2], ones_b], writes=[bank[4]])
                        kb.op(pe, lambda: PE.matmul(psf(5), lhsT=ones[:], rhs=v2[c % 2][:], start=(c == 0), stop=(c == 7)),
                              reads=[v2_b[c % 2], ones_b], writes=[bank[5]])
                    for i in range(10):
                        if i < 8:
                            c = i
                            g = g0 + c
                            G, Gb = DG[g % 2], DG_b[g % 2]
                            kb.op(pe, [(lambda k=k: PE.matmul(psf(c % 4), lhsT=G[:, k, :], rhs=U[:, c, k:k + 512],
                                                              start=(k == 0), stop=(k == KW - 1))) for k in range(KW)],
                                  reads=[U_b[c], Gb], writes=[bank[c % 4]])
                            dg_load(g + 2)
                            if side and i < len(side):
                                for f in side[i]:
                                    f()
                        if i >= 2:
                            v_st(i - 2)
                        if 1 <= i <= 8:
                            v_act(i - 1)
                    gidx[0] += 8

                def ln_norm(nxt):
                    gpe, grest = glu(nxt) if nxt is not None else (None, None)
                    kb.op(act, lambda: ACT.activation(out=mu[:], in_=psf(4), func=AF.Copy, scale=1.0 / D),
                          reads=[bank[4]], writes=[mu_b])
                    kb.op(dve, lambda: DVE.tensor_tensor(out=rs[:], in0=mu[:], in1=mu[:], op=ALU.mult),
                          reads=[mu_b], writes=[rs_b])
                    kb.op(dve, lambda: DVE.scalar_tensor_tensor(out=rs[:], in0=psf(5), scalar=1.0 / D, in1=rs[:],
                                                                op0=ALU.mult, op1=ALU.subtract),
                          reads=[bank[5], rs_b], writes=[rs_b])
                    kb.op(dve, lambda: DVE.tensor_scalar(out=rs[:], in0=rs[:], scalar1=0.0, scalar2=EPS, op0=ALU.max, op1=ALU.add),
                          reads=[rs_b], writes=[rs_b])
                    if gpe:
                        gpe(0)
                    kb.op(act, lambda: ACT.activation(out=rs[:], in_=rs[:], func=AF.Ln), reads=[rs_b], writes=[rs_b])
                    kb.op(act, lambda: ACT.activation(out=rs[:], in_=rs[:], func=AF.Exp, scale=-0.5), reads=[rs_b], writes=[rs_b])
                    if gpe:
                        gpe(1)
                        grest(0)

                    def nrm_a(c):
                        A, Ab = w1[c % 2], w1_b[c % 2]
                        kb.op(dve, lambda: DVE.tensor_tensor(out=A[:], in0=vv[:, c, :], in1=mu[:], op=ALU.subtract),
                              reads=[vv_b[c], mu_b], writes=[Ab])
                        kb.op(dve, lambda: DVE.tensor_tensor(out=A[:], in0=A[:], in1=rs[:], op=ALU.mult),
                              reads=[Ab, rs_b], writes=[Ab])

                    def nrm_b(c1):
                        A, Ab = w1[c1 % 2], w1_b[c1 % 2]
                        B, Bb = w2[c1 % 2], w2_b[c1 % 2]
                        kb.op(act, lambda: ACT.activation(out=B[:], in_=A[:], func=AF.Silu, scale=cv[:, c1, 32:33], bias=cv[:, c1, 33:34]),
                              reads=[Ab, cv_b], writes=[Bb])
                        kb.op(pool, lambda: POOL.tensor_tensor(out=GT[:, c1, :], in0=B[:], in1=SZ[:, c1, :], op=ALU.mult),
                              reads=[Bb, SZ_b[c1]], writes=[GT_b[c1]])
                    for c in range(9):
                        if c < 8:
                            nrm_a(c)
                        if c >= 1:
                            nrm_b(c - 1)
                        if gpe:
                            if c + 2 < 8:
                                gpe(c + 2)
                            if c + 1 < 8:
                                grest(c + 1)

                def outs(g):
                    s, b = GBL[g]
                    for tt in range(5):
                        if tt < 4:
                            out_tile_a(tt, 2 * (tt % 2))
                        if tt >= 1:
                            out_tile_b(src, dst, s, 4 * b + tt - 1, 2 * ((tt - 1) % 2))

                prep_block(0)
                prep_block(1)
                halo_copies(0)
                glu_plain(0)
                zgate(0)
                for g in range(NG):
                    side = prep_stages(g + 2) if g + 2 < NG else None
                    dwconv(first=(g == 0), side=side)
                    if g + 2 < NG and GBL[g + 1][0] == GBL[g + 2][0]:
                        halo_copies(g + 1)
                    ln_norm(g + 1 if g + 1 < NG else None)
                    outs(g)
                    if g + 1 < NG:
                        zgate(g + 1)

        chain = [xin, scrA, scrB, scrA, yout]
        for li, L in enumerate(layers):
            src = chain[L] if len(layers) == 4 else (xin if li == 0 else [scrA, scrB][(li - 1) % 2])
            dst = chain[L + 1] if len(layers) == 4 else (yout if li == len(layers) - 1 else [scrA, scrB][li % 2])
            if li > 0:
                kb.barrier()
                for e in kb.engs:
                    e.epoch()
            with ExitStack() as lctx:
                if L % 2 == 0:
                    attn_layer(L, L // 2, src, dst, lctx)
                else:
                    conv_layer(L, L // 2, src, dst, lctx)
                kb.barrier()
    return nc


def _rope_tables():
    rows = S // GRID_W
    row = np.repeat(np.arange(rows, dtype=np.float32), GRID_W)
    col = np.tile(np.arange(GRID_W, dtype=np.float32), rows)
    inv_freq = (ROPE_THETA ** (-np.arange(0, 64, 2, dtype=np.float32) / 64.0)).astype(np.float32)
    ang_r = row[:, None] * inv_freq[None, :]
    ang_c = col[:, None] * inv_freq[None, :]
    ang = np.concatenate([ang_r, ang_r, ang_c, ang_c], axis=-1).astype(np.float32)
    cos = np.cos(ang).astype(np.float32)
    sin = np.sin(ang).astype(np.float32)
    sgn = np.concatenate([-np.ones(32), np.ones(32), -np.ones(32), np.ones(32)]).astype(np.float32)
    return cos, (sin * sgn[None, :]).astype(np.float32)


_SWAP = np.concatenate([np.arange(32, 64), np.arange(0, 32), np.arange(96, 128), np.arange(64, 96)])


def _host_layout(inp):
    f = lambda a: np.ascontiguousarray(np.asarray(a, dtype=np.float32))
    bc = lambda v: np.ascontiguousarray(np.broadcast_to(np.asarray(v, np.float32)[:, None, :], (v.shape[0], 128, v.shape[1])))
    gq, gk = np.asarray(inp["attn_q_norm_g"], np.float32), np.asarray(inp["attn_k_norm_g"], np.float32)
    gqk = np.stack([gq, gq[:, _SWAP], gk, gk[:, _SWAP]], axis=1)
    gqk_b = np.ascontiguousarray(np.broadcast_to(gqk[:, :, None, :], (2, 4, 128, 128)))
    dw = np.asarray(inp["conv_dw_w"], np.float32)
    vec = np.concatenate([dw, np.asarray(inp["conv_dw_b"], np.float32)[:, None, :],
                          np.asarray(inp["conv_ln_g"], np.float32)[:, None, :],
                          np.asarray(inp["conv_ln_b"], np.float32)[:, None, :]], axis=1)
    cvec = np.ascontiguousarray(vec.reshape(2, 34, 8, 128).transpose(0, 3, 2, 1))
    cos, sin = _rope_tables()
    return {
        "gpre_b": bc(inp["pre_norm_g"]), "gpost_b": bc(inp["post_norm_g"]),
        "attn_w_in": f(inp["attn_w_in"]), "attn_w_out": f(inp["attn_w_out"]), "gqk_b": gqk_b,
        "conv_w_in": f(inp["conv_w_in"]), "conv_w_out": f(inp["conv_w_out"]), "conv_vec": cvec,
        "cs_t": np.ascontiguousarray(np.stack([cos, sin], axis=1)), "ident": np.eye(128, dtype=np.float32),
    }


def kernel(**inputs):
    xp = np.asarray(inputs["x_prompt"], np.float32)
    xs = np.asarray(inputs["x_sample"], np.float32)
    xall = np.concatenate([xp, xs], axis=0)
    shared = _host_layout(inputs)
    nc = build_program()
    in_maps = []
    for c in range(N_CORES):
        m = dict(shared)
        m["xin"] = np.ascontiguousarray(xall[c * SEQ_PER_CORE:(c + 1) * SEQ_PER_CORE])
        in_maps.append(m)
    res = run_bass_kernel_spmd(nc, in_maps, core_ids=list(range(N_CORES)))
    yall = np.concatenate([np.asarray(r["yout"], np.float32) for r in res.results], axis=0)
    return (np.ascontiguousarray(yall[:xp.shape[0]]), np.ascontiguousarray(yall[xp.shape[0]:]))
```

```python
import numpy as np
from contextlib import ExitStack
import concourse.bass as bass
import concourse.mybir as mybir
from concourse.bass_utils import run_bass_kernel_spmd

F32 = mybir.dt.float32
BF16 = mybir.dt.bfloat16
AF = mybir.ActivationFunctionType
ALU = mybir.AluOpType
AX = mybir.AxisListType

D = 1024
S = 4096
NT = S // 128
NB = S // 512
HD = 128
NH = 8
NKV = 2
KW = 31
PAD = 15
EPS = 1e-6
N_CORES = 8
SEQ_PER_CORE = 3
ROPE_THETA = 10000.0
GRID_W = 64
HW = 512 + 2 * PAD


class Buf:
    __slots__ = ("name", "w", "r", "dsem", "dcnt", "_wdma")

    def __init__(self, name):
        self.name = name
        self.w = None
        self.r = {}
        self.dsem = None
        self.dcnt = 0
        self._wdma = False


class Eng:
    def __init__(self, kb, name, e, self_wait=True):
        self.kb, self.name, self.e = kb, name, e
        self.sem = None
        self.cnt = 0
        self.seen = {}
        self.nep = 0
        self.self_wait = self_wait

    def epoch(self):
        self.nep += 1
        self.sem = self.kb.new_sem(f"{self.name}e{self.nep}")
        self.cnt = 0

    def mark(self, ins):
        self.cnt += 1
        ins.then_inc(self.sem, 1)
        return (self.sem, self.cnt, self)

    def wait(self, toks):
        for t in toks:
            if t is None:
                continue
            sem, v, owner = t
            if owner is self and not self.self_wait:
                continue
            k = id(sem)
            if self.seen.get(k, 0) >= v:
                continue
            self.seen[k] = v
            self.e.wait_ge(sem, v)


class KB:
    def __init__(self, nc, ctx):
        self.nc, self.ctx = nc, ctx
        self.nsem = 0
        self.pe = Eng(self, "pe", nc.tensor, self_wait=False)
        self.act = Eng(self, "act", nc.scalar)
        self.dve = Eng(self, "dve", nc.vector)
        self.pool = Eng(self, "pool", nc.gpsimd)
        self.sp = Eng(self, "sp", nc.sync)
        self.engs = [self.pe, self.act, self.dve, self.pool, self.sp]
        self.dma_bufs = []
        for e in self.engs:
            e.epoch()

    def new_sem(self, name):
        self.nsem += 1
        return self.ctx.enter_context(self.nc.semaphore(f"{name}_{self.nsem}"))

    def sb(self, name, shape, dt):
        return self.ctx.enter_context(self.nc.sbuf_tensor(name, shape, dt))

    @staticmethod
    def _deps(reads, writes):
        deps = []
        for b in reads:
            deps.append(b.w)
        for b in writes:
            deps.append(b.w)
            deps.extend(b.r.values())
        return deps

    @staticmethod
    def _note(tok, reads, writes):
        for b in reads:
            b.r[id(tok[0])] = tok
        for b in writes:
            b.w = tok
            b.r = {}

    def op(self, eng, fns, reads=(), writes=()):
        eng.wait(self._deps(reads, writes))
        if callable(fns):
            fns = [fns]
        ins = None
        for f in fns:
            ins = f()
        tok = eng.mark(ins)
        self._note(tok, reads, writes)
        for b in writes:
            b._wdma = False
        return tok

    def dma(self, eng, out, in_, sbuf, reads=(), writes=(), disjoint=False):
        if sbuf.dsem is None:
            sbuf.dsem = self.new_sem("d_" + sbuf.name)
            self.dma_bufs.append(sbuf)
        deps = []
        for b in reads:
            deps.append(b.w)
        for b in writes:
            if b.w is not None and not (disjoint and b.w[0] is sbuf.dsem and b is sbuf and b._wdma):
                deps.append(b.w)
            deps.extend(b.r.values())
        eng.wait(deps)
        sbuf.dcnt += 16
        eng.e.dma_start(out=out, in_=in_).then_inc(sbuf.dsem, 16)
        tok = (sbuf.dsem, sbuf.dcnt, None)
        self._note(tok, reads, writes)
        for b in writes:
            b._wdma = True
        return tok

    def barrier(self):
        toks = [(e.sem, e.cnt, None) for e in self.engs if e.cnt > 0]
        toks += [(b.dsem, b.dcnt, None) for b in self.dma_bufs]
        for e in self.engs:
            e.wait(toks)


def build_program(nseq=SEQ_PER_CORE, layers=(0, 1, 2, 3)):
    nc = bass.Bass("TRN2", target_bir_lowering=False)
    dr = lambda n, s, k: nc.dram_tensor(n, list(s), F32, kind=k).ap()
    xin = dr("xin", [nseq, S, D], "ExternalInput")
    yout = dr("yout", [nseq, S, D], "ExternalOutput")
    scrA = dr("scrA", [nseq, S, D], "Internal")
    scrB = dr("scrB", [nseq, S, D], "Internal")
    gpre_d = dr("gpre_b", [4, 128, D], "ExternalInput")
    gpost_d = dr("gpost_b", [4, 128, D], "ExternalInput")
    awin_d = dr("attn_w_in", [2, D, 2560], "ExternalInput")
    awout_d = dr("attn_w_out", [2, D, D], "ExternalInput")
    gqk_d = dr("gqk_b", [2, 4, 128, 128], "ExternalInput")
    cwin_d = dr("conv_w_in", [2, D, 3072], "ExternalInput")
    cwout_d = dr("conv_w_out", [2, D, D], "ExternalInput")
    cvec_d = dr("conv_vec", [2, 128, 8, 34], "ExternalInput")
    cs_d = dr("cs_t", [S, 2, 128], "ExternalInput")
    idn_d = dr("ident", [128, 128], "ExternalInput")

    ctx = ExitStack()
    with ctx:
        kb = KB(nc, ctx)
        pe, act, dve, pool, sp = kb.pe, kb.act, kb.dve, kb.pool, kb.sp
        PE, ACT, DVE, POOL = nc.tensor, nc.scalar, nc.vector, nc.gpsimd

        ps = ctx.enter_context(nc.psum_tensor("ps", [128, 4096], F32))
        bank = [Buf(f"bank{i}") for i in range(8)]
        psf = lambda b, n=1: ps[:, b * 512:(b + n) * 512]
        psb = lambda b: ps[:, b * 512:(b + 1) * 512].bitcast(BF16)

        ident = kb.sb("ident_sb", [128, 128], BF16); ident_b = Buf("ident")
        ones = kb.sb("ones", [128, 128], BF16); ones_b = Buf("ones")
        mhalf = kb.sb("mhalf", [128, 16], F32); mhalf_b = Buf("mhalf")
        gpre = kb.sb("gpre", [128, D], F32); gpre_b = Buf("gpre")
        gpost = kb.sb("gpost", [128, D], F32); gpost_b = Buf("gpost")
        WA_b = Buf("WA")
        WZ = kb.sb("WZ", [128, 8, 1024], BF16); WZ_b = Buf("WZ")
        WO = kb.sb("WO", [128, 8, 1024], BF16); WO_b = Buf("WO")
        NX = 3
        xt = [kb.sb(f"xt{i}", [128, D], F32) for i in range(NX)]; xt_b = [Buf(f"xt{i}") for i in range(NX)]
        hb = [kb.sb(f"hb{i}", [128, D], BF16) for i in range(2)]; hb_b = [Buf(f"hb{i}") for i in range(2)]
        sq = kb.sb("sq", [128, D], BF16); sq_b = Buf("sq")
        NST = 16
        st = [kb.sb(f"st{i}", [128, 8], F32) for i in range(NST)]; st_b = [Buf(f"st{i}") for i in range(NST)]
        tmp = kb.sb("tmp", [128, D], F32); tmp_b = Buf("tmp")
        yt = [kb.sb(f"yt{i}", [128, D], F32) for i in range(2)]; yt_b = [Buf(f"yt{i}") for i in range(2)]
        GT = kb.sb("GT", [128, 8, 512], BF16); GT_b = [Buf(f"GT{i}") for i in range(8)]
        SZ = kb.sb("SZ", [128, 8, 512], BF16); SZ_b = [Buf(f"SZ{i}") for i in range(8)]

        kb.dma(pool, ident[:], idn_d, ident_b, writes=[ident_b])
        kb.op(dve, lambda: DVE.memset(ones[:], 1.0), writes=[ones_b])
        kb.op(dve, lambda: DVE.memset(mhalf[:], -0.5), writes=[mhalf_b])

        cnt = {"x": 0, "hb": 0, "st": 0, "y": 0}

        def dram_bufs(tag):
            return [[Buf(f"{tag}_{s}_{t}") for t in range(NT)] for s in range(nseq)]
        dbufs = {id(xin): dram_bufs("xin"), id(scrA): dram_bufs("scrA"), id(scrB): dram_bufs("scrB"),
                 id(yout): dram_bufs("yout")}

        def load_weights(dst, dst_b, src, col0, ncol, dcol0=0):
            for kc in range(8):
                kb.dma(pool, dst[:, kc, dcol0:dcol0 + ncol], src[kc * 128:(kc + 1) * 128, col0:col0 + ncol],
                       dst_b, writes=[dst_b], disjoint=True)

        def rstd_from_ss(stt, stb, c0, n, inv_n):
            v = stt[:, c0:c0 + n]
            kb.op(act, lambda: ACT.activation(out=v, in_=v, func=AF.Ln, scale=inv_n, bias=EPS),
                  reads=[stb], writes=[stb])
            kb.op(act, lambda: ACT.activation(out=v, in_=v, func=AF.Exp, scale=-0.5),
                  reads=[stb], writes=[stb])

        def prenorm_a(src, s, t):
            i = cnt["x"] % NX; cnt["x"] += 1
            X, Xb = xt[i], xt_b[i]
            kb.dma(sp, X[:], src[s, t * 128:(t + 1) * 128, :], Xb, reads=[dbufs[id(src)][s][t]], writes=[Xb])
            j = cnt["st"] % NST; cnt["st"] += 1
            T, Tb = st[j], st_b[j]
            kb.op(act, lambda: ACT.activation(out=sq[:], in_=X[:], func=AF.Square, accum_out=T[:, 0:1]),
                  reads=[Xb], writes=[sq_b, Tb])
            rstd_from_ss(T, Tb, 0, 1, 1.0 / D)
            h = cnt["hb"] % 2; cnt["hb"] += 1
            H, Hb = hb[h], hb_b[h]
            kb.op(dve, lambda: DVE.scalar_tensor_tensor(out=H[:], in0=X[:], scalar=T[:, 0:1], in1=gpre[:],
                                                        op0=ALU.mult, op1=ALU.mult),
                  reads=[Xb, Tb, gpre_b], writes=[Hb])
            return H, Hb

        def prenorm_b(H, Hb, dstT, dst_bufs, col, pbank):
            pv = psb(pbank)
            kb.op(pe, [(lambda c=c: PE.transpose(pv[:, c * 128:(c + 1) * 128], H[:, c * 128:(c + 1) * 128], ident[:]))
                       for c in range(8)],
                  reads=[Hb, ident_b], writes=[bank[pbank]])
            kb.op(dve, lambda: DVE.tensor_copy(out=dstT[:, :, col:col + 128],
                                               in_=pv.rearrange("p (c t) -> p c t", c=8)),
                  reads=[bank[pbank]], writes=dst_bufs)

        def prenorm_tile(src, s, t, dstT, dst_bufs, col, pbank):
            H, Hb = prenorm_a(src, s, t)
            prenorm_b(H, Hb, dstT, dst_bufs, col, pbank)

        def out_tile_a(tt, pb0):
            for hf in range(2):
                kb.op(pe, [(lambda kc=kc: PE.matmul(psf(pb0 + hf), lhsT=GT[:, kc, tt * 128:(tt + 1) * 128],
                                                    rhs=WO[:, kc, hf * 512:(hf + 1) * 512],
                                                    start=(kc == 0), stop=(kc == 7))) for kc in range(8)],
                      reads=GT_b + [WO_b], writes=[bank[pb0 + hf]])

        def out_tile_b(src, dst, s, t, pb0):
            mps = psf(pb0, 2)
            j = cnt["st"] % NST; cnt["st"] += 1
            T, Tb = st[j], st_b[j]
            kb.op(act, lambda: ACT.activation(out=sq[:], in_=mps, func=AF.Square, accum_out=T[:, 0:1]),
                  reads=[bank[pb0], bank[pb0 + 1]], writes=[sq_b, Tb])
            rstd_from_ss(T, Tb, 0, 1, 1.0 / D)
            kb.op(dve, lambda: DVE.scalar_tensor_tensor(out=tmp[:], in0=mps, scalar=T[:, 0:1], in1=gpost[:],
                                                        op0=ALU.mult, op1=ALU.mult),
                  reads=[bank[pb0], bank[pb0 + 1], Tb, gpost_b], writes=[tmp_b])
            i = cnt["x"] % NX; cnt["x"] += 1
            X, Xb = xt[i], xt_b[i]
            kb.dma(sp, X[:], src[s, t * 128:(t + 1) * 128, :], Xb, reads=[dbufs[id(src)][s][t]], writes=[Xb])
            k = cnt["y"] % 2; cnt["y"] += 1
            Y, Yb = yt[k], yt_b[k]
            kb.op(dve, lambda: DVE.tensor_tensor(out=Y[:], in0=tmp[:], in1=X[:], op=ALU.add),
                  reads=[tmp_b, Xb], writes=[Yb])
            kb.dma(sp, dst[s, t * 128:(t + 1) * 128, :], Y[:], Yb, reads=[Yb], writes=[dbufs[id(dst)][s][t]])

        def out_tile(src, dst, s, t, tt, pb0):
            out_tile_a(tt, pb0)
            out_tile_b(src, dst, s, t, pb0)

        def attn_layer(L, j, src, dst, lctx):
            lsb = lambda n, shp, dt: lctx.enter_context(nc.sbuf_tensor(f"{n}_L{L}", shp, dt))
            WA = lsb("WA", [128, 8, 1536], BF16)
            KT = lsb("KT", [128, NKV, S], BF16); KT_b = [Buf(f"KT{t}") for t in range(NT)]
            V = lsb("V", [128, NT, 256], BF16); V_b = [Buf(f"V{t}") for t in range(NT)]
            gqk = lsb("gqk", [128, 4, 128], F32); gqk_b = Buf("gqk")
            HT = lsb("HT", [128, 8, 512], BF16); HT_b = [Buf(f"HT{i}") for i in range(4)]
            QT2 = [lsb(f"QT{i}", [128, 8, 512], BF16) for i in range(2)]
            QT2_b = [[Buf(f"QT{i}_{k}") for k in range(4)] for i in range(2)]
            QW = 512
            qf = lsb("qf", [128, QW], F32); qf_b = Buf("qf")
            t1 = lsb("t1", [128, QW], F32); t1_b = Buf("t1")
            t2 = lsb("t2", [128, QW], F32); t2_b = Buf("t2")
            qr = lsb("qr", [128, QW], BF16); qr_b = Buf("qr")
            NCS = 3
            cs = [lsb(f"cs{i}", [128, 2, 128], F32) for i in range(NCS)]; cs_b = [Buf(f"cs{i}") for i in range(NCS)]
            csq = [lsb(f"csq{i}", [128, 2, 128], F32) for i in range(2)]; csq_b = [Buf(f"csq{i}") for i in range(2)]
            qcnt = [0]
            kf = lsb("kf", [128, 2, 256], F32); kf_b = [Buf(f"kf{i}") for i in range(2)]
            k1 = lsb("k1", [128, 2, 256], F32); k1_b = [Buf(f"k1{i}") for i in range(2)]
            k2 = lsb("k2", [128, 2, 256], F32); k2_b = [Buf(f"k2{i}") for i in range(2)]
            kr = lsb("kr", [128, 2, 256], BF16); kr_b = [Buf(f"kr{i}") for i in range(2)]
            NPT = 4
            PT = [lsb(f"PT{i}", [128, 1024], BF16) for i in range(NPT)]; PT_b = [Buf(f"PT{i}") for i in range(NPT)]
            ld = lsb("ld", [128, 512], F32); ld_b = Buf("ld")
            on = lsb("on", [128, 512], F32); on_b = Buf("on")
            ze = lsb("ze", [128, 512], F32); ze_b = Buf("ze")
            NPS = 4
            PS2 = [lsb(f"PS{i}", [128, 512], BF16) for i in range(NPS)]; PS2_b = [Buf(f"PS{i}") for i in range(NPS)]

            load_weights(WA, WA_b, awin_d[j], 0, 1536)
            load_weights(WZ, WZ_b, awin_d[j], 1536, 1024)
            load_weights(WO, WO_b, awout_d[j], 0, 1024)
            kb.dma(sp, gpre[:], gpre_d[L], gpre_b, writes=[gpre_b])
            kb.dma(sp, gpost[:], gpost_d[L], gpost_b, writes=[gpost_b])
            kb.dma(sp, gqk[:], gqk_d[j].rearrange("a p d -> p a d"), gqk_b, writes=[gqk_b])
            ccnt = [0]

            def rope_tables(t, which):
                i = ccnt[0] % NCS; ccnt[0] += 1
                C, Cb = cs[i], cs_b[i]
                kb.dma(sp, C[:, 0:2, :], cs_d[t * 128:(t + 1) * 128, :, :], Cb, writes=[Cb])
                kb.op(pool, lambda: POOL.tensor_tensor(out=C[:, 2:4, :], in0=C[:, 0:2, :],
                                                       in1=gqk[:, 2 * which:2 * which + 2, :], op=ALU.mult),
                      reads=[Cb, gqk_b], writes=[Cb])
                return C, Cb

            def norm_rope(src_ps, src_banks, nh, C, Cb):
                W = nh * 128
                j2 = cnt["st"] % NST; cnt["st"] += 1
                T, Tb = st[j2], st_b[j2]
                kb.op(act, lambda: ACT.activation(out=sq[:, 0:W], in_=src_ps, func=AF.Square),
                      reads=src_banks, writes=[sq_b])
                kb.op(act, lambda: ACT.activation(out=qf[:, 0:W], in_=src_ps, func=AF.Copy),
                      reads=src_banks, writes=[qf_b])
                kb.op(dve, lambda: DVE.tensor_reduce(out=T[:, 0:nh], in_=sq[:, 0:W].rearrange("p (h d) -> p h d", h=nh),
                                                     op=ALU.add, axis=AX.X),
                      reads=[sq_b], writes=[Tb])
                rstd_from_ss(T, Tb, 0, nh, 1.0 / HD)
                q3 = qf[:, 0:W].rearrange("p (h d) -> p h d", h=nh)
                kb.op(dve, lambda: DVE.tensor_tensor(out=q3, in0=q3, in1=T[:, 0:nh].unsqueeze(2).to_broadcast([128, nh, 128]),
                                                     op=ALU.mult),
                      reads=[qf_b, Tb], writes=[qf_b])
                kb.op(dve, lambda: DVE.tensor_tensor(out=t1[:, 0:W].rearrange("p (h d) -> p h d", h=nh), in0=q3,
                                                     in1=C[:, 2:3, :].to_broadcast([128, nh, 128]), op=ALU.mult),
                      reads=[qf_b, Cb], writes=[t1_b])
                q4 = qf[:, 0:W].rearrange("p (h r f e) -> p h r f e", h=nh, r=2, f=2)
                o4 = t2[:, 0:W].rearrange("p (h r f e) -> p h r f e", h=nh, r=2, f=2)
                s4 = C[:, 3, :].rearrange("p (r f e) -> p r f e", r=2, f=2)
                fns = []
                for f in range(2):
                    fns.append(lambda f=f: POOL.tensor_tensor(
                        out=o4[:, :, :, f, :], in0=q4[:, :, :, 1 - f, :],
                        in1=s4[:, :, f, :].unsqueeze(1).to_broadcast([128, nh, 2, 32]), op=ALU.mult))
                kb.op(pool, fns, reads=[qf_b, Cb], writes=[t2_b])
                kb.op(dve, lambda: DVE.tensor_tensor(out=qr[:, 0:W], in0=t1[:, 0:W], in1=t2[:, 0:W], op=ALU.add),
                      reads=[t1_b, t2_b], writes=[qr_b])

            scale = float(HD) ** -0.5
            for s in range(nseq):
                PB = 6

                def prologue_stages(b, stride=17):
                    nb = b % 2
                    stages = []
                    for tt in range(4):
                        t = 4 * b + tt
                        o = tt * stride
                        stv = {}

                        def L(t=t, stv=stv):
                            i = cnt["x"] % NX; cnt["x"] += 1
                            stv["X"] = (xt[i], xt_b[i])
                            kb.dma(sp, xt[i][:], src[s, t * 128:(t + 1) * 128, :], xt_b[i],
                                   reads=[dbufs[id(src)][s][t]], writes=[xt_b[i]])
                            ci = qcnt[0] % 2; qcnt[0] += 1
                            stv["C"] = (csq[ci], csq_b[ci])
                            kb.dma(sp, csq[ci][:], cs_d[t * 128:(t + 1) * 128, :, :], csq_b[ci], writes=[csq_b[ci]])

                        def N1(stv=stv):
                            X, Xb = stv["X"]
                            j2 = cnt["st"] % NST; cnt["st"] += 1
                            T, Tb = st[j2], st_b[j2]
                            stv["T"] = (T, Tb)
                            kb.op(act, lambda: ACT.activation(out=sq[:], in_=X[:], func=AF.Square, accum_out=T[:, 0:1]),
                                  reads=[Xb], writes=[sq_b, Tb])

                        def N1b(stv=stv):
                            T, Tb = stv["T"]
                            rstd_from_ss(T, Tb, 0, 1, 1.0 / D)

                        def N2(stv=stv):
                            X, Xb = stv["X"]; T, Tb = stv["T"]; C, Cb = stv["C"]
                            h = cnt["hb"] % 2; cnt["hb"] += 1
                            H, Hb = hb[h], hb_b[h]
                            stv["H"] = (H, Hb)
                            kb.op(dve, lambda: DVE.scalar_tensor_tensor(out=H[:], in0=X[:], scalar=T[:, 0:1], in1=gpre[:],
                                                                        op0=ALU.mult, op1=ALU.mult),
                                  reads=[Xb, Tb, gpre_b], writes=[Hb])
                            kb.op(pool, lambda: POOL.tensor_tensor(out=C[:], in0=C[:], in1=gqk[:, 0:2, :], op=ALU.mult),
                                  reads=[Cb, gqk_b], writes=[Cb])

                        def T1(stv=stv):
                            H, Hb = stv["H"]
                            pv = psb(PB)
                            kb.op(pe, [(lambda c=c: PE.transpose(pv[:, c * 128:(c + 1) * 128], H[:, c * 128:(c + 1) * 128], ident[:]))
                                       for c in range(8)], reads=[Hb, ident_b], writes=[bank[PB]])

                        def C1(tt=tt):
                            pv = psb(PB)
                            kb.op(dve, lambda: DVE.tensor_copy(out=HT[:, :, tt * 128:(tt + 1) * 128],
                                                               in_=pv.rearrange("p (c t) -> p c t", c=8)),
                                  reads=[bank[PB]], writes=[HT_b[tt]])
                        stages += [(o + 0, L), (o + 4, N1), (o + 5, N1b), (o + 6, N2), (o + 7, T1), (o + 8, C1)]
                        for hf in range(2):
                            oo = o + 9 + 5 * hf

                            def Q1(tt=tt, hf=hf):
                                kb.op(pe, [(lambda kc=kc: PE.matmul(psf(PB + 1), lhsT=HT[:, kc, tt * 128:(tt + 1) * 128],
                                                                    rhs=WA[:, kc, hf * 512:(hf + 1) * 512],
                                                                    start=(kc == 0), stop=(kc == 7))) for kc in range(8)],
                                      reads=[HT_b[tt], WA_b], writes=[bank[PB + 1]])

                            def R1a(stv=stv):
                                j2 = cnt["st"] % NST; cnt["st"] += 1
                                T, Tb = st[j2], st_b[j2]
                                stv["T2"] = (T, Tb)
                                src_ps = psf(PB + 1)
                                kb.op(act, [(lambda h=h: ACT.activation(out=sq[:, h * 128:(h + 1) * 128], in_=src_ps[:, h * 128:(h + 1) * 128],
                                                                        func=AF.Square, accum_out=T[:, h:h + 1])) for h in range(2)],
                                      reads=[bank[PB + 1]], writes=[sq_b, Tb])

                            def R1b(stv=stv):
                                T, Tb = stv["T2"]
                                src_ps = psf(PB + 1)
                                kb.op(act, [(lambda h=h: ACT.activation(out=sq[:, h * 128:(h + 1) * 128], in_=src_ps[:, h * 128:(h + 1) * 128],
                                                                        func=AF.Square, accum_out=T[:, h:h + 1])) for h in range(2, 4)],
                                      reads=[bank[PB + 1]], writes=[sq_b, Tb])

                            def R1c(stv=stv):
                                kb.op(act, lambda: ACT.activation(out=qf[:], in_=psf(PB + 1), func=AF.Copy),
                                      reads=[bank[PB + 1]], writes=[qf_b])

                            def R1d(stv=stv):
                                T, Tb = stv["T2"]
                                rstd_from_ss(T, Tb, 0, 4, 1.0 / HD)

                            def R2(stv=stv):
                                T, Tb = stv["T2"]; C, Cb = stv["C"]
                                q3 = qf[:].rearrange("p (h d) -> p h d", h=4)
                                kb.op(dve, lambda: DVE.tensor_tensor(out=q3, in0=q3, in1=T[:, 0:4].unsqueeze(2).to_broadcast([128, 4, 128]),
                                                                     op=ALU.mult), reads=[qf_b, Tb], writes=[qf_b])
                                kb.op(dve, lambda: DVE.tensor_tensor(out=t1[:].rearrange("p (h d) -> p h d", h=4), in0=q3,
                                                                     in1=C[:, 0:1, :].to_broadcast([128, 4, 128]), op=ALU.mult),
                                      reads=[qf_b, Cb], writes=[t1_b])

                            def R2b(stv=stv):
                                C, Cb = stv["C"]
                                q4 = qf[:].rearrange("p (h r f e) -> p h r f e", h=4, r=2, f=2)
                                o4 = t2[:].rearrange("p (h r f e) -> p h r f e", h=4, r=2, f=2)
                                s4 = C[:, 1, :].rearrange("p (r f e) -> p r f e", r=2, f=2)
                                kb.op(pool, [(lambda f=f: POOL.tensor_tensor(
                                    out=o4[:, :, :, f, :], in0=q4[:, :, :, 1 - f, :],
                                    in1=s4[:, :, f, :].unsqueeze(1).to_broadcast([128, 4, 2, 32]), op=ALU.mult)) for f in range(2)],
                                      reads=[qf_b, Cb], writes=[t2_b])

                            def R3():
                                kb.op(dve, lambda: DVE.tensor_tensor(out=qr[:], in0=t1[:], in1=t2[:], op=ALU.add),
                                      reads=[t1_b, t2_b], writes=[qr_b])

                            def T2():
                                pv = psb(PB)
                                kb.op(pe, [(lambda h=h: PE.transpose(pv[:, h * 128:(h + 1) * 128], qr[:, h * 128:(h + 1) * 128], ident[:]))
                                           for h in range(4)], reads=[qr_b, ident_b], writes=[bank[PB]])

                            def C2(tt=tt, hf=hf):
                                pv = psb(PB)
                                kb.op(dve, lambda: DVE.tensor_copy(out=QT2[nb][:, 4 * hf:4 * hf + 4, tt * 128:(tt + 1) * 128],
                                                                   in_=pv[:, 0:512].rearrange("p (h t) -> p h t", h=4)),
                                      reads=[bank[PB]], writes=[QT2_b[nb][tt]])
                            stages += [(oo, Q1), (oo + 2, R1a), (oo + 3, R1b), (oo + 4, R1c), (oo + 5, R1d), (oo + 6, R2), (oo + 6, R2b),
                                       (oo + 7, R3), (oo + 8, T2), (oo + 9, C2)]
                    stages.sort(key=lambda x: x[0])
                    return stages

                def z_stages(fc):
                    zb = PB + (fc % 2)

                    def Z1():
                        kb.op(pe, [(lambda kc=kc: PE.matmul(psf(zb), lhsT=WZ[:, kc, fc * 128:(fc + 1) * 128], rhs=HT[:, kc, :],
                                                            start=(kc == 0), stop=(kc == 7))) for kc in range(8)],
                              reads=HT_b + [WZ_b], writes=[bank[zb]])

                    def Z2a():
                        kb.op(act, lambda: ACT.activation(out=ze[:], in_=psf(zb), func=AF.Exp, scale=-1.0),
                              reads=[bank[zb]], writes=[ze_b])

                    def Z2b():
                        kb.op(act, lambda: ACT.activation(out=ze[:], in_=ze[:], func=AF.Ln, bias=1.0), reads=[ze_b], writes=[ze_b])

                    def Z2c():
                        kb.op(act, lambda: ACT.activation(out=ze[:], in_=ze[:], func=AF.Exp, scale=-1.0), reads=[ze_b], writes=[ze_b])

                    def Z3():
                        kb.op(dve, lambda: DVE.tensor_tensor(out=SZ[:, fc, :], in0=psf(zb), in1=ze[:], op=ALU.mult),
                              reads=[bank[zb], ze_b], writes=[SZ_b[fc]])
                    return [(0, Z1), (2, Z2a), (3, Z2b), (4, Z2c), (5, Z3)]

                def epilogue_stages(b, tt):
                    t = 4 * b + tt
                    stv = {}

                    def Ea():
                        i = cnt["x"] % NX; cnt["x"] += 1
                        stv["X"] = (xt[i], xt_b[i])
                        kb.dma(sp, xt[i][:], src[s, t * 128:(t + 1) * 128, :], xt_b[i],
                               reads=[dbufs[id(src)][s][t]], writes=[xt_b[i]])
                        out_tile_a(tt, PB)

                    def Eb1():
                        j2 = cnt["st"] % NST; cnt["st"] += 1
                        T, Tb = st[j2], st_b[j2]
                        stv["T"] = (T, Tb)
                        kb.op(act, lambda: ACT.activation(out=sq[:], in_=psf(PB, 2), func=AF.Square, accum_out=T[:, 0:1]),
                              reads=[bank[PB], bank[PB + 1]], writes=[sq_b, Tb])

                    def Eb1b():
                        T, Tb = stv["T"]
                        rstd_from_ss(T, Tb, 0, 1, 1.0 / D)

                    def Eb2():
                        T, Tb = stv["T"]
                        kb.op(dve, lambda: DVE.scalar_tensor_tensor(out=tmp[:], in0=psf(PB, 2), scalar=T[:, 0:1], in1=gpost[:],
                                                                    op0=ALU.mult, op1=ALU.mult),
                              reads=[bank[PB], bank[PB + 1], Tb, gpost_b], writes=[tmp_b])

                    def Eb3():
                        X, Xb = stv["X"]
                        k = cnt["y"] % 2; cnt["y"] += 1
                        Y, Yb = yt[k], yt_b[k]
                        kb.op(dve, lambda: DVE.tensor_tensor(out=Y[:], in0=tmp[:], in1=X[:], op=ALU.add),
                              reads=[tmp_b, Xb], writes=[Yb])
                        kb.dma(sp, dst[s, t * 128:(t + 1) * 128, :], Y[:], Yb, reads=[Yb], writes=[dbufs[id(dst)][s][t]])
                    return [(0, Ea), (3, Eb1), (4, Eb1b), (5, Eb2), (6, Eb3)]

                items = [(h, g) for h in range(NH) for g in range(16)]
                cur = {}

                def emit_S(b, n):
                    h, g = items[n]
                    sl = n % 2
                    kvh = h // 4
                    QT, QT_b = QT2[b % 2], QT2_b[b % 2]
                    kb.op(pe, [(lambda i=i: PE.matmul(psf(2 * sl + i), lhsT=KT[:, kvh, (2 * g + i) * 128:(2 * g + i + 1) * 128],
                                                      rhs=QT[:, h, :], start=True, stop=True)) for i in range(2)],
                          reads=QT_b + [KT_b[2 * g], KT_b[2 * g + 1]], writes=[bank[2 * sl], bank[2 * sl + 1]])

                def emit_den(n):
                    h, g = items[n]
                    P, Pb = PT[n % NPT], PT_b[n % NPT]
                    kb.op(pe, [(lambda i=i: PE.matmul(psf(5), lhsT=ones[:], rhs=P[:, i * 512:(i + 1) * 512],
                                                      start=(g == 0 and i == 0), stop=(g == 15 and i == 1))) for i in range(2)],
                          reads=[Pb, ones_b], writes=[bank[5]])
                    if g == 15:
                        kb.op(dve, lambda: DVE.tensor_copy(out=on[:], in_=psf(4)), reads=[bank[4]], writes=[on_b])
                        kb.op(act, lambda: ACT.activation(out=ld[:], in_=psf(5), func=AF.Ln), reads=[bank[5]], writes=[ld_b])
                        def fin_exp():
                            kb.op(act, lambda: ACT.activation(out=ld[:], in_=ld[:], func=AF.Exp, scale=-1.0),
                                  reads=[ld_b], writes=[ld_b])

                        def fin_mul():
                            kb.op(dve, lambda: DVE.tensor_tensor(out=on[:], in0=on[:], in1=ld[:], op=ALU.mult),
                                  reads=[on_b, ld_b], writes=[on_b])

                        def gt_write(h=h):
                            kb.op(pool, lambda: POOL.tensor_tensor(out=GT[:, h, :], in0=on[:], in1=SZ[:, h, :], op=ALU.mult),
                                  reads=[on_b, SZ_b[h]], writes=[GT_b[h]])
                        pos = cur["n"]
                        cur["at"](pos + 1, fin_exp)
                        cur["at"](pos + 2, fin_mul)
                        if h == 0 and cur["b"] > 0:
                            cur["at"](max(27, pos + 2), gt_write)
                        else:
                            cur["at"](pos + 2, gt_write)

                def emit_rest(b, n):
                    h, g = items[n]
                    sl = n % 2
                    kvh = h // 4
                    P, Pb = PT[n % NPT], PT_b[n % NPT]
                    Q, Qb = PS2[n % NPS], PS2_b[n % NPS]
                    kb.op(act, lambda: ACT.activation(out=P[:], in_=psf(2 * sl, 2), func=AF.Exp, scale=scale),
                          reads=[bank[2 * sl], bank[2 * sl + 1]], writes=[Pb])
                    cur["n"] = n
                    if n >= 1:
                        emit_den(n - 1)
                    fns = []
                    for i in range(2):
                        kc = 2 * g + i
                        fns.append(lambda i=i, kc=kc: PE.matmul(psf(4), lhsT=V[:, kc, kvh * 128:(kvh + 1) * 128],
                                                                rhs=P[:, i * 512:(i + 1) * 512],
                                                                start=(kc == 0), stop=(kc == NT - 1)))
                    kb.op(pe, fns, reads=[Pb, V_b[2 * g], V_b[2 * g + 1]], writes=[bank[4]])

                stA = {}

                def aL(t):
                    i = cnt["x"] % NX; cnt["x"] += 1
                    stA[(t, "X")] = (xt[i], xt_b[i])
                    kb.dma(sp, xt[i][:], src[s, t * 128:(t + 1) * 128, :], xt_b[i],
                           reads=[dbufs[id(src)][s][t]], writes=[xt_b[i]])

                def aN1(t):
                    X, Xb = stA[(t, "X")]
                    j2 = cnt["st"] % NST; cnt["st"] += 1
                    T, Tb = st[j2], st_b[j2]
                    stA[(t, "T")] = (T, Tb)
                    kb.op(act, lambda: ACT.activation(out=sq[:], in_=X[:], func=AF.Square, accum_out=T[:, 0:1]),
                          reads=[Xb], writes=[sq_b, Tb])
                    rstd_from_ss(T, Tb, 0, 1, 1.0 / D)

                def aN2(t):
                    X, Xb = stA.pop((t, "X")); T, Tb = stA.pop((t, "T"))
                    h = cnt["hb"] % 2; cnt["hb"] += 1
                    H, Hb = hb[h], hb_b[h]
                    stA[(t, "H")] = (H, Hb)
                    kb.op(dve, lambda: DVE.scalar_tensor_tensor(out=H[:], in0=X[:], scalar=T[:, 0:1], in1=gpre[:],
                                                                op0=ALU.mult, op1=ALU.mult),
                          reads=[Xb, Tb, gpre_b], writes=[Hb])

                def aT1(t):
                    H, Hb = stA.pop((t, "H"))
                    pv = psb(t % 2)
                    kb.op(pe, [(lambda c=c: PE.transpose(pv[:, c * 128:(c + 1) * 128], H[:, c * 128:(c + 1) * 128], ident[:]))
                               for c in range(8)], reads=[Hb, ident_b], writes=[bank[t % 2]])

                def aC1(t):
                    par = t % 2
                    kb.op(dve, lambda: DVE.tensor_copy(out=HT[:, :, par * 128:(par + 1) * 128],
                                                       in_=psb(par).rearrange("p (c t) -> p c t", c=8)),
                          reads=[bank[par]], writes=[HT_b[par]])

                def aK1(t):
                    par = t % 2
                    kb.op(pe, [(lambda kc=kc: PE.matmul(psf(2 + par), lhsT=HT[:, kc, par * 128:(par + 1) * 128],
                                                        rhs=WA[:, kc, 1024:1536], start=(kc == 0), stop=(kc == 7)))
                               for kc in range(8)], reads=[HT_b[par], WA_b], writes=[bank[2 + par]])
                    ci = ccnt[0] % NCS; ccnt[0] += 1
                    stA[(t, "C")] = (cs[ci], cs_b[ci])
                    kb.dma(sp, cs[ci][:], cs_d[t * 128:(t + 1) * 128, :, :], cs_b[ci], writes=[cs_b[ci]])

                def aR1(t):
                    par = t % 2
                    kps = psf(2 + par)
                    C, Cb = stA[(t, "C")]
                    kb.op(pool, lambda: POOL.tensor_tensor(out=C[:], in0=C[:], in1=gqk[:, 2:4, :], op=ALU.mult),
                          reads=[Cb, gqk_b], writes=[Cb])
                    kb.op(act, lambda: ACT.activation(out=V[:, t, :], in_=kps[:, 256:512], func=AF.Copy),
                          reads=[bank[2 + par]], writes=[V_b[t]])
                    j2 = cnt["st"] % NST; cnt["st"] += 1
                    T, Tb = st[j2], st_b[j2]
                    stA[(t, "T2")] = (T, Tb)
                    kb.op(act, [(lambda h=h: ACT.activation(out=sq[:, h * 128:(h + 1) * 128], in_=kps[:, h * 128:(h + 1) * 128],
                                                            func=AF.Square, accum_out=T[:, h:h + 1])) for h in range(2)],
                          reads=[bank[2 + par]], writes=[sq_b, Tb])
                    kb.op(act, lambda: ACT.activation(out=kf[:, par, :], in_=kps[:, 0:256], func=AF.Copy),
                          reads=[bank[2 + par]], writes=[kf_b[par]])
                    rstd_from_ss(T, Tb, 0, 2, 1.0 / HD)

                def aR2(t):
                    par = t % 2
                    T, Tb = stA.pop((t, "T2")); C, Cb = stA.pop((t, "C"))
                    q3 = kf[:, par, :].rearrange("p (h d) -> p h d", h=2)
                    kb.op(dve, lambda: DVE.tensor_tensor(out=q3, in0=q3, in1=T[:, 0:2].unsqueeze(2).to_broadcast([128, 2, 128]),
                                                         op=ALU.mult), reads=[kf_b[par], Tb], writes=[kf_b[par]])
                    kb.op(dve, lambda: DVE.tensor_tensor(out=k1[:, par, :].rearrange("p (h d) -> p h d", h=2), in0=q3,
                                                         in1=C[:, 0:1, :].to_broadcast([128, 2, 128]), op=ALU.mult),
                          reads=[kf_b[par], Cb], writes=[k1_b[par]])
                    q4 = kf[:, par, :].rearrange("p (h r f e) -> p h r f e", h=2, r=2, f=2)
                    o4 = k2[:, par, :].rearrange("p (h r f e) -> p h r f e", h=2, r=2, f=2)
                    s4 = C[:, 1, :].rearrange("p (r f e) -> p r f e", r=2, f=2)
                    kb.op(pool, [(lambda f=f: POOL.tensor_tensor(
                        out=o4[:, :, :, f, :], in0=q4[:, :, :, 1 - f, :],
                        in1=s4[:, :, f, :].unsqueeze(1).to_broadcast([128, 2, 2, 32]), op=ALU.mult)) for f in range(2)],
                          reads=[kf_b[par], Cb], writes=[k2_b[par]])

                def aR3(t):
                    par = t % 2
                    kb.op(dve, lambda: DVE.tensor_tensor(out=kr[:, par, :], in0=k1[:, par, :], in1=k2[:, par, :], op=ALU.add),
                          reads=[k1_b[par], k2_b[par]], writes=[kr_b[par]])

                def aT2(t):
                    par = t % 2
                    pv = psb(4 + par)
                    kb.op(pe, [(lambda g=g: PE.transpose(pv[:, g * 128:(g + 1) * 128], kr[:, par, g * 128:(g + 1) * 128], ident[:]))
                               for g in range(2)], reads=[kr_b[par], ident_b], writes=[bank[4 + par]])

                def aC2(t):
                    par = t % 2
                    kb.op(dve, lambda: DVE.tensor_copy(out=KT[:, :, t * 128:(t + 1) * 128],
                                                       in_=psb(4 + par)[:, 0:256].rearrange("p (g t) -> p g t", g=2)),
                          reads=[bank[4 + par]], writes=[KT_b[t]])

                a_stages = [aL, aN1, aN2, aT1, aC1, aK1, aR1, aR2, aR3, aT2, aC2]
                NIT = NT + len(a_stages) - 1
                pro0 = prologue_stages(0)
                pi = 0
                for it in range(NIT):
                    for k in range(len(a_stages) - 1, -1, -1):
                        t = it - k
                        if 0 <= t < NT:
                            a_stages[k](t)
                while pi < len(pro0):
                    pro0[pi][1]()
                    pi += 1

                for fc in range(8):
                    for _, f in z_stages(fc):
                        f()

                for b in range(NB):
                    sched = {}

                    def at(pos, f):
                        sched.setdefault(min(pos, 127), []).append(f)
                    cur["b"] = b
                    cur["at"] = at
                    if b > 0:
                        for o, f in z_stages(7):
                            at(0 + o, f)
                        for tt in range(4):
                            for o, f in epilogue_stages(b - 1, tt):
                                at(6 + 6 * tt + o, f)
                    if b + 1 < NB:
                        for o, f in prologue_stages(b + 1):
                            at(31 + o, f)
                        for fc in range(7):
                            for o, f in z_stages(fc):
                                at(107 + 3 * fc + o, f)
                    emit_S(b, 0)
                    for n in range(len(items)):
                        if n + 1 < len(items):
                            emit_S(b, n + 1)
                        emit_rest(b, n)
                        if n == len(items) - 1:
                            cur["n"] = n + 1
                            emit_den(n)
                        for f in sched.get(n, []):
                            f()
                for tt in range(4):
                    for _, f in epilogue_stages(NB - 1, tt):
                        f()

        def conv_layer(L, j, src, dst, lctx):
            lsb = lambda n, shp, dt: lctx.enter_context(nc.sbuf_tensor(f"{n}_L{L}", shp, dt))
            WA = lsb("WA", [128, 8, 2048], BF16)
            DG = [lsb(f"DG{i}", [128, KW, 128], BF16) for i in range(2)]; DG_b = [Buf(f"DG{i}") for i in range(2)]
            cv = lsb("cv", [128, 8, 34], F32); cv_b = Buf("cv")
            idf = lsb("idf", [128, 128], F32); idf_b = Buf("idf")
            Hh = [lsb(f"Hh{i}", [128, 8, HW], BF16) for i in range(2)]
            Hh_b = [[Buf(f"Hh{i}_{k}") for k in range(6)] for i in range(2)]
            U = lsb("U", [128, 8, HW], BF16); U_b = [Buf(f"U{c}") for c in range(8)]
            th = [lsb(f"th{i}", [128, HW], F32) for i in range(2)]; th_b = [Buf(f"th{i}") for i in range(2)]
            vv = lsb("vv", [128, 8, 512], F32); vv_b = [Buf(f"vv{c}") for c in range(8)]
            vb = [lsb(f"vb{i}", [128, 512], BF16) for i in range(2)]; vb_b = [Buf(f"vb{i}") for i in range(2)]
            v2 = [lsb(f"v2{i}", [128, 512], BF16) for i in range(2)]; v2_b = [Buf(f"v2{i}") for i in range(2)]
            mu = lsb("mu", [128, 512], F32); mu_b = Buf("mu")
            rs = lsb("rs", [128, 512], F32); rs_b = Buf("rs")
            w1 = [lsb(f"w1{i}", [128, 512], F32) for i in range(2)]; w1_b = [Buf(f"w1{i}") for i in range(2)]
            w2 = [lsb(f"w2{i}", [128, 512], F32) for i in range(2)]; w2_b = [Buf(f"w2{i}") for i in range(2)]
            dgd = nc.dram_tensor(f"dgd_L{L}", [8, 128, KW * 128], BF16, kind="Internal").ap()
            dgd_b = [Buf(f"dgd{c}") for c in range(8)]

            load_weights(WA, WA_b, cwin_d[j], 0, 2048)
            load_weights(WZ, WZ_b, cwin_d[j], 2048, 1024)
            load_weights(WO, WO_b, cwout_d[j], 0, 1024)
            kb.dma(sp, gpre[:], gpre_d[L], gpre_b, writes=[gpre_b])
            kb.dma(sp, gpost[:], gpost_d[L], gpost_b, writes=[gpost_b])
            kb.dma(sp, cv[:], cvec_d[j], cv_b, writes=[cv_b])
            kb.dma(sp, idf[:], idn_d, idf_b, writes=[idf_b])
            for c in range(8):
                G, Gb = DG[c % 2], DG_b[c % 2]
                kb.op(dve, [(lambda k=k: DVE.tensor_scalar(out=G[:, k, :], in0=idf[:], scalar1=cv[:, c, k:k + 1],
                                                           scalar2=0.5, op0=ALU.mult, op1=ALU.mult)) for k in range(KW)],
                      reads=[idf_b, cv_b], writes=[Gb])
                kb.dma(sp, dgd[c], G[:].rearrange("p k m -> p (k m)"), dgd_b[c], reads=[Gb], writes=[dgd_b[c]])
            gidx = [0]

            def dg_load(g):
                c = g % 8
                kb.dma(pool, DG[g % 2][:].rearrange("p k m -> p (k m)"), dgd[c], DG_b[g % 2],
                       reads=[dgd_b[c]], writes=[DG_b[g % 2]])

            GBL = [(s_, b_) for s_ in range(nseq) for b_ in range(NB)]
            NG = len(GBL)
            if True:
                def prep_stages(g):
                    s, b = GBL[g]
                    i = g % 2
                    hs = {}

                    def A(tt):
                        hs[tt] = prenorm_a(src, s, 4 * b + tt)

                    def B(tt):
                        H_, Hb_ = hs.pop(tt)
                        prenorm_b(H_, Hb_, Hh[i], [Hh_b[i][tt]], PAD + tt * 128, 6 + tt % 2)
                    return [[lambda: A(0), lambda: A(1)], [lambda: B(0), lambda: A(2)], [lambda: B(1), lambda: A(3)],
                            [lambda: B(2)], [lambda: B(3)]]

                def prep_block(g):
                    for st_ in prep_stages(g):
                        for f in st_:
                            f()

                def halo_copies(g):
                    i = g % 2
                    kb.op(pool, lambda: POOL.tensor_copy(out=Hh[i][:, :, PAD + 512:HW], in_=Hh[1 - i][:, :, PAD:2 * PAD]),
                          reads=[Hh_b[1 - i][0]], writes=[Hh_b[i][5]])
                    kb.op(pool, lambda: POOL.tensor_copy(out=Hh[1 - i][:, :, 0:PAD], in_=Hh[i][:, :, 512:512 + PAD]),
                          reads=[Hh_b[i][3]], writes=[Hh_b[1 - i][4]])

                def glu(g):
                    s, b = GBL[g]
                    H, Hb = Hh[g % 2], Hh_b[g % 2]
                    has = (b > 0, b + 1 < NB)

                    def pe_part(c):
                        ab, gb_ = 2 * (c % 2), 2 * (c % 2) + 1
                        hbk = 6 + (c % 2)

                        def mm(dst_ps, col0, kc, rhs):
                            return PE.matmul(dst_ps, lhsT=WA[:, kc, col0 + c * 128:col0 + (c + 1) * 128], rhs=rhs,
                                             start=(kc == 0), stop=(kc == 7))
                        kb.op(pe, [(lambda kc=kc: mm(psf(ab), 0, kc, H[:, kc, PAD:PAD + 512])) for kc in range(8)],
                              reads=Hb[0:4] + [WA_b], writes=[bank[ab]])
                        kb.op(pe, [(lambda kc=kc: mm(psf(gb_), 1024, kc, H[:, kc, PAD:PAD + 512])) for kc in range(8)],
                              reads=Hb[0:4] + [WA_b], writes=[bank[gb_]])
                        for side, hc0 in ((0, 0), (1, PAD + 512)):
                            if not has[side]:
                                continue
                            pa = psf(hbk)[:, side * 128:side * 128 + PAD]
                            pg = psf(hbk)[:, side * 128 + 32:side * 128 + 32 + PAD]
                            kb.op(pe, [(lambda kc=kc: mm(pa, 0, kc, H[:, kc, hc0:hc0 + PAD])) for kc in range(8)] +
                                      [(lambda kc=kc: mm(pg, 1024, kc, H[:, kc, hc0:hc0 + PAD])) for kc in range(8)],
                                  reads=[Hb[4 + side], WA_b], writes=[bank[hbk]])

                    def rest(c):
                        ab, gb_ = 2 * (c % 2), 2 * (c % 2) + 1
                        hbk = 6 + (c % 2)
                        T, Tb = th[c % 2], th_b[c % 2]
                        kb.op(act, lambda: ACT.activation(out=T[:, PAD:PAD + 512], in_=psf(gb_), func=AF.Tanh, scale=0.5),
                              reads=[bank[gb_]], writes=[Tb])
                        for side, hc0 in ((0, 0), (1, PAD + 512)):
                            if not has[side]:
                                kb.op(pool, lambda: POOL.memset(U[:, c, hc0:hc0 + PAD], 0.0), writes=[U_b[c]])
                                continue
                            pg = psf(hbk)[:, side * 128 + 32:side * 128 + 32 + PAD]
                            kb.op(act, lambda: ACT.activation(out=T[:, hc0:hc0 + PAD], in_=pg, func=AF.Tanh, scale=0.5),
                                  reads=[bank[hbk]], writes=[Tb])
                        kb.op(dve, lambda: DVE.scalar_tensor_tensor(out=U[:, c, PAD:PAD + 512], in0=T[:, PAD:PAD + 512], scalar=1.0,
                                                                    in1=psf(ab), op0=ALU.add, op1=ALU.mult),
                              reads=[Tb, bank[ab]], writes=[U_b[c]])
                        for side, hc0 in ((0, 0), (1, PAD + 512)):
                            if not has[side]:
                                continue
                            pa = psf(hbk)[:, side * 128:side * 128 + PAD]
                            kb.op(dve, lambda: DVE.scalar_tensor_tensor(
                                out=U[:, c, hc0:hc0 + PAD], in0=T[:, hc0:hc0 + PAD], scalar=1.0, in1=pa, op0=ALU.add, op1=ALU.mult),
                                  reads=[Tb, bank[hbk]], writes=[U_b[c]])
                    return pe_part, rest

                def glu_plain(g):
                    pe_part, rest = glu(g)
                    for c in range(9):
                        if c < 8:
                            pe_part(c)
                        if c >= 1:
                            rest(c - 1)

                def zgate(g):
                    H, Hb = Hh[g % 2], Hh_b[g % 2]
                    for fc in range(9):
                        if fc < 8:
                            zb = 6 + fc % 2
                            kb.op(pe, [(lambda kc=kc: PE.matmul(psf(zb), lhsT=WZ[:, kc, fc * 128:(fc + 1) * 128],
                                                                rhs=H[:, kc, PAD:PAD + 512], start=(kc == 0), stop=(kc == 7)))
                                       for kc in range(8)], reads=Hb[0:4] + [WZ_b], writes=[bank[zb]])
                        if fc >= 1:
                            f1 = fc - 1
                            kb.op(act, lambda: ACT.activation(out=SZ[:, f1, :], in_=psf(6 + f1 % 2), func=AF.Silu),
                                  reads=[bank[6 + f1 % 2]], writes=[SZ_b[f1]])

                def dwconv(first, side):
                    if first:
                        dg_load(gidx[0]); dg_load(gidx[0] + 1)
                    g0 = gidx[0]

                    def v_act(c):
                        cb = c % 4
                        kb.op(act, lambda: ACT.activation(out=vv[:, c, :], in_=psf(cb), func=AF.Identity, bias=cv[:, c, 31:32]),
                              reads=[bank[cb], cv_b], writes=[vv_b[c]])
                        kb.op(act, lambda: ACT.activation(out=v2[c % 2][:], in_=psf(cb), func=AF.Square, bias=cv[:, c, 31:32]),
                              reads=[bank[cb], cv_b], writes=[v2_b[c % 2]])
                        kb.op(act, lambda: ACT.activation(out=vb[c % 2][:], in_=psf(cb), func=AF.Identity, bias=cv[:, c, 31:32]),
                              reads=[bank[cb], cv_b], writes=[vb_b[c % 2]])

                    def v_st(c):
                        kb.op(pe, lambda: PE.matmul(psf(4), lhsT=ones[:], rhs=vb[c % 2][:], start=(c == 0), stop=(c == 7)),
                              reads=[vb_b[c % 2], ones_b], writes=[bank[4]])
                        kb.op(pe, lambda: PE.matmul(psf(5), lhsT=ones[:], rhs=v2[c % 2][:], start=(c == 0), stop=(c == 7)),
                              reads=[v2_b[c % 2], ones_b], writes=[bank[5]])
                    for i in range(10):
                        if i < 8:
                            c = i
                            g = g0 + c
                            G, Gb = DG[g % 2], DG_b[g % 2]
                            kb.op(pe, [(lambda k=k: PE.matmul(psf(c % 4), lhsT=G[:, k, :], rhs=U[:, c, k:k + 512],
                                                              start=(k == 0), stop=(k == KW - 1))) for k in range(KW)],
                                  reads=[U_b[c], Gb], writes=[bank[c % 4]])
                            dg_load(g + 2)
                            if side and i < len(side):
                                for f in side[i]:
                                    f()
                        if i >= 2:
                            v_st(i - 2)
                        if 1 <= i <= 8:
                            v_act(i - 1)
                    gidx[0] += 8

                def ln_norm(nxt):
                    gpe, grest = glu(nxt) if nxt is not None else (None, None)
                    kb.op(act, lambda: ACT.activation(out=mu[:], in_=psf(4), func=AF.Copy, scale=1.0 / D),
                          reads=[bank[4]], writes=[mu_b])
                    kb.op(dve, lambda: DVE.tensor_tensor(out=rs[:], in0=mu[:], in1=mu[:], op=ALU.mult),
                          reads=[mu_b], writes=[rs_b])
                    kb.op(dve, lambda: DVE.scalar_tensor_tensor(out=rs[:], in0=psf(5), scalar=1.0 / D, in1=rs[:],
                                                                op0=ALU.mult, op1=ALU.subtract),
                          reads=[bank[5], rs_b], writes=[rs_b])
                    kb.op(dve, lambda: DVE.tensor_scalar(out=rs[:], in0=rs[:], scalar1=0.0, scalar2=EPS, op0=ALU.max, op1=ALU.add),
                          reads=[rs_b], writes=[rs_b])
                    if gpe:
                        gpe(0)
                    kb.op(act, lambda: ACT.activation(out=rs[:], in_=rs[:], func=AF.Ln), reads=[rs_b], writes=[rs_b])
                    kb.op(act, lambda: ACT.activation(out=rs[:], in_=rs[:], func=AF.Exp, scale=-0.5), reads=[rs_b], writes=[rs_b])
                    if gpe:
                        gpe(1)
                        grest(0)

                    def nrm_a(c):
                        A, Ab = w1[c % 2], w1_b[c % 2]
                        kb.op(dve, lambda: DVE.tensor_tensor(out=A[:], in0=vv[:, c, :], in1=mu[:], op=ALU.subtract),
                              reads=[vv_b[c], mu_b], writes=[Ab])
                        kb.op(dve, lambda: DVE.tensor_tensor(out=A[:], in0=A[:], in1=rs[:], op=ALU.mult),
                              reads=[Ab, rs_b], writes=[Ab])

                    def nrm_b(c1):
                        A, Ab = w1[c1 % 2], w1_b[c1 % 2]
                        B, Bb = w2[c1 % 2], w2_b[c1 % 2]
                        kb.op(act, lambda: ACT.activation(out=B[:], in_=A[:], func=AF.Silu, scale=cv[:, c1, 32:33], bias=cv[:, c1, 33:34]),
                              reads=[Ab, cv_b], writes=[Bb])
                        kb.op(pool, lambda: POOL.tensor_tensor(out=GT[:, c1, :], in0=B[:], in1=SZ[:, c1, :], op=ALU.mult),
                              reads=[Bb, SZ_b[c1]], writes=[GT_b[c1]])
                    for c in range(9):
                        if c < 8:
                            nrm_a(c)
                        if c >= 1:
                            nrm_b(c - 1)
                        if gpe:
                            if c + 2 < 8:
                                gpe(c + 2)
                            if c + 1 < 8:
                                grest(c + 1)

                def outs(g):
                    s, b = GBL[g]
                    for tt in range(5):
                        if tt < 4:
                            out_tile_a(tt, 2 * (tt % 2))
                        if tt >= 1:
                            out_tile_b(src, dst, s, 4 * b + tt - 1, 2 * ((tt - 1) % 2))

                prep_block(0)
                prep_block(1)
                halo_copies(0)
                glu_plain(0)
                zgate(0)
                for g in range(NG):
                    side = prep_stages(g + 2) if g + 2 < NG else None
                    dwconv(first=(g == 0), side=side)
                    if g + 2 < NG and GBL[g + 1][0] == GBL[g + 2][0]:
                        halo_copies(g + 1)
                    ln_norm(g + 1 if g + 1 < NG else None)
                    outs(g)
                    if g + 1 < NG:
                        zgate(g + 1)

        chain = [xin, scrA, scrB, scrA, yout]
        for li, L in enumerate(layers):
            src = chain[L] if len(layers) == 4 else (xin if li == 0 else [scrA, scrB][(li - 1) % 2])
            dst = chain[L + 1] if len(layers) == 4 else (yout if li == len(layers) - 1 else [scrA, scrB][li % 2])
            if li > 0:
                kb.barrier()
                for e in kb.engs:
                    e.epoch()
            with ExitStack() as lctx:
                if L % 2 == 0:
                    attn_layer(L, L // 2, src, dst, lctx)
                else:
                    conv_layer(L, L // 2, src, dst, lctx)
                kb.barrier()
    return nc


def _rope_tables():
    rows = S // GRID_W
    row = np.repeat(np.arange(rows, dtype=np.float32), GRID_W)
    col = np.tile(np.arange(GRID_W, dtype=np.float32), rows)
    inv_freq = (ROPE_THETA ** (-np.arange(0, 64, 2, dtype=np.float32) / 64.0)).astype(np.float32)
    ang_r = row[:, None] * inv_freq[None, :]
    ang_c = col[:, None] * inv_freq[None, :]
    ang = np.concatenate([ang_r, ang_r, ang_c, ang_c], axis=-1).astype(np.float32)
    cos = np.cos(ang).astype(np.float32)
    sin = np.sin(ang).astype(np.float32)
    sgn = np.concatenate([-np.ones(32), np.ones(32), -np.ones(32), np.ones(32)]).astype(np.float32)
    return cos, (sin * sgn[None, :]).astype(np.float32)


_SWAP = np.concatenate([np.arange(32, 64), np.arange(0, 32), np.arange(96, 128), np.arange(64, 96)])


def _host_layout(inp):
    f = lambda a: np.ascontiguousarray(np.asarray(a, dtype=np.float32))
    bc = lambda v: np.ascontiguousarray(np.broadcast_to(np.asarray(v, np.float32)[:, None, :], (v.shape[0], 128, v.shape[1])))
    gq, gk = np.asarray(inp["attn_q_norm_g"], np.float32), np.asarray(inp["attn_k_norm_g"], np.float32)
    gqk = np.stack([gq, gq[:, _SWAP], gk, gk[:, _SWAP]], axis=1)
    gqk_b = np.ascontiguousarray(np.broadcast_to(gqk[:, :, None, :], (2, 4, 128, 128)))
    dw = np.asarray(inp["conv_dw_w"], np.float32)
    vec = np.concatenate([dw, np.asarray(inp["conv_dw_b"], np.float32)[:, None, :],
                          np.asarray(inp["conv_ln_g"], np.float32)[:, None, :],
                          np.asarray(inp["conv_ln_b"], np.float32)[:, None, :]], axis=1)
    cvec = np.ascontiguousarray(vec.reshape(2, 34, 8, 128).transpose(0, 3, 2, 1))
    cos, sin = _rope_tables()
    return {
        "gpre_b": bc(inp["pre_norm_g"]), "gpost_b": bc(inp["post_norm_g"]),
        "attn_w_in": f(inp["attn_w_in"]), "attn_w_out": f(inp["attn_w_out"]), "gqk_b": gqk_b,
        "conv_w_in": f(inp["conv_w_in"]), "conv_w_out": f(inp["conv_w_out"]), "conv_vec": cvec,
        "cs_t": np.ascontiguousarray(np.stack([cos, sin], axis=1)), "ident": np.eye(128, dtype=np.float32),
    }


def kernel(**inputs):
    xp = np.asarray(inputs["x_prompt"], np.float32)
    xs = np.asarray(inputs["x_sample"], np.float32)
    xall = np.concatenate([xp, xs], axis=0)
    shared = _host_layout(inputs)
    nc = build_program()
    in_maps = []
    for c in range(N_CORES):
        m = dict(shared)
        m["xin"] = np.ascontiguousarray(xall[c * SEQ_PER_CORE:(c + 1) * SEQ_PER_CORE])
        in_maps.append(m)
    res = run_bass_kernel_spmd(nc, in_maps, core_ids=list(range(N_CORES)))
    yall = np.concatenate([np.asarray(r["yout"], np.float32) for r in res.results], axis=0)
    return (np.ascontiguousarray(yall[:xp.shape[0]]), np.ascontiguousarray(yall[xp.shape[0]:]))
```
